# Optimizing a Trainium2 kernel written in Bass

```python
import jax, jax.numpy as jnp
from jax import lax
import numpy as np

D_MODEL = 1024
BATCH = 8
SEQ = 2048
DEPTH = 2
DEC_BATCH = 128
DEC_SEQ = 1
PAST_LEN = 16384
PAGE_SIZE = 128

N_BRANCH = 4
BRANCH_WIDTH = D_MODEL // N_BRANCH
N_HEADS = 4
HEAD_DIM = BRANCH_WIDTH // N_HEADS
D_FF = ((8 * D_MODEL // 3 + 255) // 256) * 256
CHUNK = 64
CONV_WIDTH = 4
LRU_C = 8.0
POOL_WINDOWS = (2, 4, 8, 16)
POOL_GROUP = BRANCH_WIDTH // len(POOL_WINDOWS)
POOL_BUF = max(POOL_WINDOWS) - 1
ROPE_BASE = 10000.0
N_MIX_SLOTS = 11
IN_COLS = N_MIX_SLOTS * BRANCH_WIDTH + N_BRANCH * D_MODEL
EPS = 1e-6
F_FLOOR = 1e-30

kernel_name = 'hybrid_hgrn2_rglru_retnet_pool_decoder_step'


def rms_norm(x, g):
    xf = x.astype(jnp.float32)
    y = xf * lax.rsqrt(jnp.mean(xf * xf, axis=-1, keepdims=True) + EPS)
    return (y * g.astype(jnp.float32)).astype(x.dtype)


def swiglu_ffn(x, g, w_up, w_down):
    a, b = jnp.split(rms_norm(x, g) @ w_up, 2, axis=-1)
    return (jax.nn.silu(a) * b) @ w_down


def to_heads(a):
    B, T, _ = a.shape
    return a.reshape(B, T, N_HEADS, HEAD_DIM).transpose(0, 2, 1, 3)


def from_heads(a):
    B, H, T, Dh = a.shape
    return a.transpose(0, 2, 1, 3).reshape(B, T, H * Dh)


def head_rms_norm(o):
    return o * lax.rsqrt(jnp.mean(o * o, axis=-1, keepdims=True) + EPS)


def head_group_norm(o):
    mu = jnp.mean(o, axis=-1, keepdims=True)
    var = jnp.mean(jnp.square(o - mu), axis=-1, keepdims=True)
    return (o - mu) * lax.rsqrt(var + EPS)


def chunk_linear_recurrence(q, k, v, log_f, s0):
    B, H, T, _ = q.shape
    c = CHUNK if T % CHUNK == 0 else T
    n = T // c

    def blocks(a):
        return a.reshape(B, H, n, c, a.shape[-1]).transpose(2, 0, 1, 3, 4)

    causal = jnp.tril(jnp.ones((c, c), dtype=bool))[None, None, :, :, None]

    def step(S, inp):
        qc, kc, vc, gc = inp
        cum = jnp.cumsum(gc, axis=2)
        o_inter = jnp.einsum('bhtk,bhkv->bhtv', qc * jnp.exp(cum), S)
        diff = cum[:, :, :, None, :] - cum[:, :, None, :, :]
        w = jnp.where(causal, jnp.exp(jnp.minimum(diff, 0.0)), 0.0)
        scores = jnp.sum(qc[:, :, :, None, :] * kc[:, :, None, :, :] * w, axis=-1)
        o = o_inter + jnp.einsum('bhts,bhsv->bhtv', scores, vc)
        last = cum[:, :, -1:, :]
        S = (jnp.exp(last[:, :, 0, :])[..., None] * S
             + jnp.einsum('bhsk,bhsv->bhkv', kc * jnp.exp(last - cum), vc))
        return S, o

    s_final, o = lax.scan(step, s0, (blocks(q), blocks(k), blocks(v), blocks(log_f)))
    o = o.transpose(1, 2, 0, 3, 4).reshape(B, H, T, v.shape[-1])
    return o, s_final


def hgrn2_mixer(uq, uf, ui, ug, lb, norm_g, s0):
    q = jax.nn.silu(uq.astype(jnp.float32))
    z = uf.astype(jnp.float32)
    f = lb + (1.0 - lb) * jax.nn.sigmoid(z)
    log_f = jnp.log(jnp.maximum(f, F_FLOOR))
    k = (1.0 - lb) * jax.nn.sigmoid(-z)
    o, s = chunk_linear_recurrence(to_heads(q), to_heads(k), to_heads(ui.astype(jnp.float32)),
                                   to_heads(log_f), s0.astype(jnp.float32))
    o = from_heads(head_rms_norm(o)) * norm_g * jax.nn.silu(ug.astype(jnp.float32))
    return o, s


def rglru_mixer(ux, uy, conv_w, conv_b, w_r, b_r, w_i, b_i, lam, h0, conv0):
    xf = ux.astype(jnp.float32)
    B, T, W = xf.shape
    xp = jnp.concatenate([conv0.astype(jnp.float32), xf], axis=1)
    xc = conv_b + sum(conv_w[j] * xp[:, j:j + T] for j in range(CONV_WIDTH))
    new_conv = xp[:, T:]
    xh = xc.reshape(B, T, N_HEADS, HEAD_DIM)
    r = jax.nn.sigmoid(jnp.einsum('bthi,hij->bthj', xh, w_r).reshape(B, T, W) + b_r)
    i = jax.nn.sigmoid(jnp.einsum('bthi,hij->bthj', xh, w_i).reshape(B, T, W) + b_i)
    log_a = -LRU_C * r * jax.nn.softplus(-lam)
    a = jnp.exp(log_a)
    b_in = jnp.sqrt(jnp.maximum(-jnp.expm1(2.0 * log_a), 0.0)) * (i * xc)

    def combine(left, right):
        a1, b1 = left
        a2, b2 = right
        return a1 * a2, a2 * b1 + b2

    a_cum, b_cum = lax.associative_scan(combine, (a, b_in), axis=1)
    h = a_cum * h0.astype(jnp.float32)[:, None] + b_cum
    out = h * jax.nn.gelu(uy.astype(jnp.float32))
    return out, h[:, -1], new_conv


def rope(x, pos):
    half = HEAD_DIM // 2
    freq = ROPE_BASE ** (-jnp.arange(half, dtype=jnp.float32) / half)
    ang = pos[:, None] * freq[None, :]
    cos, sin = jnp.cos(ang), jnp.sin(ang)
    x1, x2 = x[..., :half], x[..., half:]
    return jnp.concatenate([x1 * cos - x2 * sin, x1 * sin + x2 * cos], axis=-1)


def retention_mixer(uq, uk, uv, ug, norm_g, s0, pos0):
    B, T, _ = uq.shape
    pos = pos0 + jnp.arange(T, dtype=jnp.float32)
    q = rope(to_heads(uq.astype(jnp.float32)), pos)
    k = rope(to_heads(uk.astype(jnp.float32)), pos) * HEAD_DIM ** -0.5
    v = to_heads(uv.astype(jnp.float32))
    log_gamma = jnp.log1p(-(2.0 ** (-5.0 - jnp.arange(N_HEADS, dtype=jnp.float32))))
    log_f = jnp.broadcast_to(log_gamma[None, :, None, None], (B, N_HEADS, T, 1))
    o, s = chunk_linear_recurrence(q, k, v, log_f, s0.astype(jnp.float32))
    o = from_heads(head_group_norm(o)) * norm_g * jax.nn.silu(ug.astype(jnp.float32))
    return o, s


def pool_mixer(ud, w_pool, scale, buf0, pos0):
    xf = ud.astype(jnp.float32)
    B, T, W = xf.shape
    xp = jnp.concatenate([buf0.astype(jnp.float32), xf], axis=1)
    cs = jnp.concatenate([jnp.zeros((B, 1, W), jnp.float32), jnp.cumsum(xp, axis=1)], axis=1)
    end = cs[:, POOL_BUF + 1:]
    pos = pos0 + jnp.arange(T, dtype=jnp.float32)
    outs = []
    for gi, win in enumerate(POOL_WINDOWS):
        sl = slice(gi * POOL_GROUP, (gi + 1) * POOL_GROUP)
        start = cs[:, POOL_BUF + 1 - win:POOL_BUF + 1 - win + T, sl]
        count = jnp.minimum(float(win), pos + 1.0)[None, :, None]
        pooled = (end[..., sl] - start) / count
        outs.append((pooled - xf[..., sl]) @ w_pool[gi])
    return jnp.concatenate(outs, axis=-1) * scale, xp[:, T:]


def trunk(x, s_hgrn, s_lru, s_conv, s_ret, s_pool, pos0, params):
    (lb_logits, ffn1_norm, ffn1_up, ffn1_down, mix_norm, w_in, hgrn_norm, conv_w, conv_b,
     w_rgate, b_rgate, w_igate, b_igate, lru_lambda, ret_norm, w_pool, pool_scale,
     w_branch, w_o, ffn2_norm, ffn2_up, ffn2_down, final_norm) = params
    B, T, _ = x.shape
    lb_soft = jax.nn.softmax(lb_logits.astype(jnp.float32), axis=0)
    lower_bounds = jnp.cumsum(lb_soft, axis=0) - lb_soft[0:1]
    new_states = ([], [], [], [], [])
    for l in range(DEPTH):
        x = x + 0.5 * swiglu_ffn(x, ffn1_norm[l], ffn1_up[l], ffn1_down[l])
        u = rms_norm(x, mix_norm[l]) @ w_in[l]
        aq, af, ai, ag, bx, by, cq, ck, cv, cg, dx = jnp.split(
            u[..., :N_MIX_SLOTS * BRANCH_WIDTH], N_MIX_SLOTS, axis=-1)
        gates = jax.nn.sigmoid(u[..., N_MIX_SLOTS * BRANCH_WIDTH:].astype(jnp.float32)
                               ).reshape(B, T, N_BRANCH, D_MODEL)
        o_a, st_a = hgrn2_mixer(aq, af, ai, ag, lower_bounds[l], hgrn_norm[l], s_hgrn[l])
        o_b, st_b, st_c = rglru_mixer(bx, by, conv_w[l], conv_b[l], w_rgate[l], b_rgate[l],
                                      w_igate[l], b_igate[l], lru_lambda[l], s_lru[l], s_conv[l])
        o_c, st_r = retention_mixer(cq, ck, cv, cg, ret_norm[l], s_ret[l], pos0)
        o_d, st_p = pool_mixer(dx, w_pool[l], pool_scale[l], s_pool[l], pos0)
        branches = (o_a, o_b, o_c, o_d)
        merged = sum(gates[:, :, b] * (branches[b] @ w_branch[l, b]) for b in range(N_BRANCH))
        x = x + (merged @ w_o[l]).astype(x.dtype)
        x = x + 0.5 * swiglu_ffn(x, ffn2_norm[l], ffn2_up[l], ffn2_down[l])
        for lst, st in zip(new_states, (st_a, st_b, st_c, st_r, st_p)):
            lst.append(st)
    y = rms_norm(x, final_norm)
    return (y, jnp.stack(new_states[0]), jnp.stack(new_states[1]), jnp.stack(new_states[2]),
            jnp.stack(new_states[3]), jnp.stack(new_states[4]))


def setup_inputs(seed: int = 0) -> dict:
    key = jax.random.key(seed)
    ks = iter(jax.random.split(key, 48))
    f32 = jnp.float32

    def nrm(shape, scale):
        return scale * jax.random.normal(next(ks), shape, f32)

    def gain(shape):
        return 1.0 + 0.05 * jax.random.normal(next(ks), shape, f32)

    W, H, Dh, L = BRANCH_WIDTH, N_HEADS, HEAD_DIM, DEPTH
    u = jax.random.uniform(next(ks), (L, W), f32, 0.9, 0.999)
    sig = u ** (1.0 / LRU_C)
    lru_lambda = jnp.log(sig) - jnp.log1p(-sig)
    return {
        'x_prompt': nrm((BATCH, SEQ, D_MODEL), 1.0),
        'x_sample': nrm((DEC_BATCH, DEC_SEQ, D_MODEL), 1.0),
        'state_hgrn': nrm((L, DEC_BATCH, H, Dh, Dh), 0.3),
        'state_rglru': nrm((L, DEC_BATCH, W), 0.5),
        'state_conv': nrm((L, DEC_BATCH, CONV_WIDTH - 1, W), 1.0),
        'state_retention': nrm((L, DEC_BATCH, H, Dh, Dh), 1.0),
        'state_pool': nrm((L, DEC_BATCH, POOL_BUF, W), 1.0),
        'lb_logits': nrm((L, W), 1.0),
        'ffn1_norm': gain((L, D_MODEL)),
        'ffn1_up': nrm((L, D_MODEL, 2 * D_FF), D_MODEL ** -0.5),
        'ffn1_down': nrm((L, D_FF, D_MODEL), D_FF ** -0.5),
        'mix_norm': gain((L, D_MODEL)),
        'w_in': nrm((L, D_MODEL, IN_COLS), D_MODEL ** -0.5),
        'hgrn_norm': gain((L, W)),
        'conv_w': nrm((L, CONV_WIDTH, W), CONV_WIDTH ** -0.5),
        'conv_b': nrm((L, W), 0.02),
        'w_rgate': nrm((L, H, Dh, Dh), Dh ** -0.5),
        'b_rgate': nrm((L, W), 0.02),
        'w_igate': nrm((L, H, Dh, Dh), Dh ** -0.5),
        'b_igate': nrm((L, W), 0.02),
        'lru_lambda': lru_lambda,
        'ret_norm': gain((L, W)),
        'w_pool': nrm((L, len(POOL_WINDOWS), POOL_GROUP, POOL_GROUP), POOL_GROUP ** -0.5),
        'pool_scale': gain((L, W)),
        'w_branch': nrm((L, N_BRANCH, W, D_MODEL), W ** -0.5),
        'w_o': nrm((L, D_MODEL, D_MODEL), D_MODEL ** -0.5),
        'ffn2_norm': gain((L, D_MODEL)),
        'ffn2_up': nrm((L, D_MODEL, 2 * D_FF), D_MODEL ** -0.5),
        'ffn2_down': nrm((L, D_FF, D_MODEL), D_FF ** -0.5),
        'final_norm': gain((D_MODEL,)),
    }


def reference(x_prompt, x_sample, state_hgrn, state_rglru, state_conv, state_retention, state_pool,
              lb_logits, ffn1_norm, ffn1_up, ffn1_down, mix_norm, w_in, hgrn_norm, conv_w, conv_b,
              w_rgate, b_rgate, w_igate, b_igate, lru_lambda, ret_norm, w_pool, pool_scale,
              w_branch, w_o, ffn2_norm, ffn2_up, ffn2_down, final_norm):
    params = (lb_logits, ffn1_norm, ffn1_up, ffn1_down, mix_norm, w_in, hgrn_norm, conv_w, conv_b,
              w_rgate, b_rgate, w_igate, b_igate, lru_lambda, ret_norm, w_pool, pool_scale,
              w_branch, w_o, ffn2_norm, ffn2_up, ffn2_down, final_norm)
    f32 = jnp.float32
    bp = x_prompt.shape[0]
    y_prompt, hgrn_p, rglru_p, conv_p, ret_p, pool_p = trunk(
        x_prompt,
        jnp.zeros((DEPTH, bp, N_HEADS, HEAD_DIM, HEAD_DIM), f32),
        jnp.zeros((DEPTH, bp, BRANCH_WIDTH), f32),
        jnp.zeros((DEPTH, bp, CONV_WIDTH - 1, BRANCH_WIDTH), f32),
        jnp.zeros((DEPTH, bp, N_HEADS, HEAD_DIM, HEAD_DIM), f32),
        jnp.zeros((DEPTH, bp, POOL_BUF, BRANCH_WIDTH), f32),
        0, params)
    y_sample, hgrn_s, rglru_s, conv_s, ret_s, pool_s = trunk(
        x_sample, state_hgrn, state_rglru, state_conv, state_retention, state_pool,
        PAST_LEN, params)
    return (y_prompt, y_sample, hgrn_p, rglru_p, conv_p, ret_p, pool_p,
            hgrn_s, rglru_s, conv_s, ret_s, pool_s)
```

```python
from contextlib import ExitStack
import numpy as np
import ml_dtypes
import concourse.bass as bass
import concourse.mybir as mybir
from concourse.bass_utils import run_bass_kernel_spmd

F32 = mybir.dt.float32
BF16 = mybir.dt.bfloat16
AF = mybir.ActivationFunctionType
ALU = mybir.AluOpType
AX = mybir.AxisListType

ENGS = ("pe", "act", "dve", "pool", "sp")
SEM_LIMIT = 30000
DMA_POOL = 12

D = 1024
NP_ = 2048
NS = 16
NT = NP_ + NS
DFF = 2816
BW = 256
L = 2
EPS = 1e-6
TILES = [(0, 512), (512, 1024), (1024, 1536), (1536, 2048), (2048, 2064)]
NCORES = 8


class Buf:
    __slots__ = ("name", "w", "r")

    def __init__(self, name=""):
        self.name = name
        self.w = None
        self.r = {}


class Op:
    __slots__ = ("eng", "fn", "dma", "idx", "deps", "signal", "sem", "val", "prev_same_sem")

    def __init__(self, eng, fn, dma, idx):
        self.eng = eng
        self.fn = fn
        self.dma = dma
        self.idx = idx
        self.deps = []
        self.signal = False
        self.sem = None
        self.val = 0
        self.prev_same_sem = None


class Sched:
    def __init__(self, nc):
        self.nc = nc
        self.ops = {e: [] for e in ENGS}
        self.dma_ops = {e: [] for e in ENGS}
        self.dma_since_barrier = {e: [] for e in ENGS}
        self._cap = None
        self.sim_free = {e: 0.0 for e in ENGS}
        self.sim_tag = None
        self.sim_w = {}
        self.sim_r = {}

    def capture(self, f):
        assert self._cap is None
        self._cap = []
        f()
        lst = self._cap
        self._cap = None
        return lst

    def _sim_start(self, eng, reads, writes):
        t = self.sim_free[eng]
        for b in reads:
            t = max(t, self.sim_w.get(id(b), 0.0))
        for b in writes:
            t = max(t, self.sim_w.get(id(b), 0.0), self.sim_r.get(id(b), 0.0))
        return t

    def _sim_commit(self, eng, reads, writes, cost):
        st = self._sim_start(eng, reads, writes)
        en = st + cost
        self.sim_free[eng] = st + (0.05 if eng in ("sp",) else cost)
        for b in reads:
            self.sim_r[id(b)] = max(self.sim_r.get(id(b), 0.0), en)
        for b in writes:
            self.sim_w[id(b)] = en + 0.15
            self.sim_r[id(b)] = 0.0

    def schedule(self, lst):
        n = len(lst)
        lastw, readers = {}, {}
        preds = [set() for _ in range(n)]
        lastdma = {}
        for i, a in enumerate(lst):
            eng, reads, writes, dma = a[0], a[2], a[3], a[4]
            for b in reads:
                if id(b) in lastw:
                    preds[i].add(lastw[id(b)])
            for b in writes:
                if id(b) in lastw:
                    preds[i].add(lastw[id(b)])
                for r in readers.get(id(b), ()):
                    preds[i].add(r)
            if dma:
                if eng in lastdma:
                    preds[i].add(lastdma[eng])
                lastdma[eng] = i
            for b in reads:
                readers.setdefault(id(b), []).append(i)
            for b in writes:
                lastw[id(b)] = i
                readers[id(b)] = []
            preds[i].discard(i)
        succs = [[] for _ in range(n)]
        indeg = [0] * n
        for i in range(n):
            indeg[i] = len(preds[i])
            for p in preds[i]:
                succs[p].append(i)
        blev = [0.0] * n
        for i in range(n - 1, -1, -1):
            m = 0.0
            for j in succs[i]:
                if blev[j] > m:
                    m = blev[j]
            blev[i] = m + lst[i][6] + 0.15
        ready = [i for i in range(n) if indeg[i] == 0]
        done = 0
        while ready:
            best, bk = None, None
            for i in ready:
                a = lst[i]
                st = self._sim_start(a[0], a[2], a[3])
                if a[0] == "act" and a[7] is not None and a[7] != self.sim_tag:
                    st += 1.3
                key = (int(st / 0.4), -blev[i], i)
                if bk is None or key < bk:
                    best, bk = i, key
            ready.remove(best)
            self.op(*lst[best])
            done += 1
            for j in succs[best]:
                indeg[j] -= 1
                if indeg[j] == 0:
                    ready.append(j)
        assert done == n

    def replay(self, *lists):
        pos = [0] * len(lists)
        tot = sum(len(x) for x in lists)
        for _ in range(tot):
            best, bk = None, None
            for i, x in enumerate(lists):
                if pos[i] < len(x):
                    a = x[pos[i]]
                    key = (self._sim_start(a[0], a[2], a[3]), pos[i] / len(x))
                    if bk is None or key < bk:
                        best, bk = i, key
            a = lists[best][pos[best]]
            pos[best] += 1
            self.op(*a)

    def op(self, eng, fn, reads=(), writes=(), dma=False, extra=(), cost=None, tag=None, nobar=False):
        if cost is None:
            cost = 2.0 if dma else {"pe": 0.25, "act": 0.6, "dve": 0.6, "pool": 1.0, "sp": 0.05}[eng]
        if self._cap is not None:
            self._cap.append((eng, fn, tuple(reads), tuple(writes), dma, tuple(extra), cost, tag, nobar))
            return None
        if tag is not None and eng == "act":
            if self.sim_tag != tag:
                cost = cost + 1.3
                self.sim_tag = tag
        self._sim_commit(eng, reads, writes, cost)
        o = Op(eng, fn, dma, len(self.ops[eng]))
        deps = {}
        for b in reads:
            if b.w is not None:
                deps[id(b.w)] = b.w
        for b in writes:
            if b.w is not None:
                deps[id(b.w)] = b.w
            for r in b.r.values():
                deps[id(r)] = r
        for d in extra:
            deps[id(d)] = d
        for d in deps.values():
            if d is o:
                continue
            if d.eng == "pe" and eng == "pe" and not d.dma and not dma:
                continue
            d.signal = True
            o.deps.append(d)
        for b in reads:
            key = (eng, o.idx) if dma else eng
            b.r[key] = o
        for b in writes:
            b.w = o
            b.r = {}
        if dma:
            o.signal = True
            lst = self.dma_ops[eng]
            n = len(lst)
            if n >= DMA_POOL:
                o.prev_same_sem = lst[n - DMA_POOL]
            lst.append(o)
            if not nobar:
                self.dma_since_barrier[eng].append(o)
        self.ops[eng].append(o)
        return o

    def barrier(self, scratch):
        mk = {
            "act": lambda e: e.activation(out=scratch["act"], in_=scratch["src"], func=AF.Copy),
            "dve": lambda e: e.tensor_copy(out=scratch["dve"], in_=scratch["src"]),
            "pool": lambda e: e.tensor_copy(out=scratch["pool"], in_=scratch["src"]),
            "sp": lambda e: e.nop(),
            "pe": lambda en: en.matmul(scratch["ps"], lhsT=scratch["mm"], rhs=scratch["mm"], start=True, stop=True),
        }
        firsts = []
        for e in ENGS:
            ex = list(self.dma_since_barrier[e])
            self.dma_since_barrier[e] = []
            wr = list(scratch["w"][e]) + (list(scratch["pe_writes"]) if e == "pe" else [])
            o = self.op(e, mk[e], extra=ex, reads=scratch["reads"], writes=wr)
            o.signal = True
            firsts.append(o)
        for e in ENGS:
            wr = list(scratch["w"][e]) + (list(scratch["pe_writes"]) if e == "pe" else [])
            self.op(e, mk[e], extra=[f for f in firsts if f.eng != e], reads=scratch["reads"], writes=wr)

    def finish(self):
        o = Op("sp", lambda e: e.nop(), False, len(self.ops["sp"]))
        for e in ENGS:
            for d in self.dma_ops[e]:
                o.deps.append(d)
        self.ops["sp"].append(o)

    def emit(self, es):
        nc = self.nc
        nsem = {}
        for e in ENGS:
            cnt = sum(1 for o in self.ops[e] if o.signal and not o.dma)
            nsem[e] = max(1, (cnt + SEM_LIMIT - 1) // SEM_LIMIT)
        sems = {e: [es.enter_context(nc.semaphore(f"s_{e}_{i}")) for i in range(nsem[e])] for e in ENGS}
        dsems = {}
        for e in ENGS:
            if self.dma_ops[e]:
                dsems[e] = [es.enter_context(nc.semaphore(f"d_{e}_{i}")) for i in range(DMA_POOL)]
        for e in ENGS:
            c = 0
            for o in self.ops[e]:
                if o.dma:
                    continue
                if o.signal:
                    o.sem = sems[e][c // SEM_LIMIT]
                    o.val = c % SEM_LIMIT + 1
                    c += 1
            for n, o in enumerate(self.dma_ops[e]):
                o.sem = dsems[e][n % DMA_POOL]
                o.val = 16 * (n // DMA_POOL + 1)
        block = es.enter_context(nc.Block())
        starter = {"pe": block.tensor, "act": block.scalar, "dve": block.vector, "pool": block.gpsimd,
                   "sp": block.sync}
        for e in ENGS:
            ops = self.ops[e]
            if not ops:
                continue

            def body(eng_obj, ops=ops):
                known = {}
                for o in ops:
                    waits = {}
                    dl = o.deps
                    if o.prev_same_sem is not None:
                        dl = dl + [o.prev_same_sem]
                    for d in dl:
                        k = id(d.sem)
                        if known.get(k, 0) >= d.val:
                            continue
                        if k not in waits or waits[k][1] < d.val:
                            waits[k] = (d.sem, d.val)
                    for k, (sem, val) in waits.items():
                        eng_obj.wait_ge(sem, val)
                        known[k] = val
                    inst = o.fn(eng_obj)
                    if o.signal:
                        inst.then_inc(o.sem, 16 if o.dma else 1)

            starter[e](body)


def _const_tables():
    c = {}
    c["ident_f"] = np.eye(128, dtype=np.float32)
    c["ident_b"] = np.eye(128, dtype=np.float32).astype(ml_dtypes.bfloat16)
    c["ones_b"] = np.ones((128, 128), np.float32).astype(ml_dtypes.bfloat16)
    bo = np.zeros((128, 128), np.float32)
    bo[:64, :64] = 1
    bo[64:, 64:] = 1
    c["bones_b"] = bo.astype(ml_dtypes.bfloat16)
    s = np.arange(128)
    t = np.arange(128)
    c["cmask"] = (t[None, :] >= s[:, None]).astype(np.float32)
    hm = np.zeros((128, 2), np.float32)
    hm[:64, 0] = 1
    hm[64:, 1] = 1
    c["hmask"] = hm
    pm = np.zeros((128, 128), np.float32)
    for p in range(128):
        pm[p, p ^ 32] = 1.0
    c["permf"] = pm
    rm = np.ones((128, 512), np.float32)
    rm[:, ::64] = 0
    c["rmask"] = rm.astype(ml_dtypes.bfloat16)
    c["zmask"] = np.zeros((128, 16), np.float32)
    pos = np.concatenate([np.arange(NP_, dtype=np.float32), np.full(NS, 16384.0, np.float32)])
    j = np.concatenate([np.arange(NP_) % 64, np.zeros(NS)]).astype(np.float64)
    half = 32
    freq = (10000.0 ** (-np.arange(half, dtype=np.float32) / half)).astype(np.float32)
    tabs = np.zeros((2, 4, 128, NT), np.float32)
    gl = np.zeros((2, 128, 3), np.float32)
    for hp in range(2):
        for p in range(128):
            h = 2 * hp + p // 64
            d = p % 64
            ang = (pos * freq[d % 32]).astype(np.float32)
            cs = np.cos(ang.astype(np.float64))
            sn = np.sin(ang.astype(np.float64))
            sgn = -1.0 if d < 32 else 1.0
            lg = np.log1p(-(2.0 ** (-5.0 - h)))
            d1 = np.exp(lg * (j + 1))
            d3 = np.exp(-lg * (j + 1)) / 8.0
            tabs[hp, 0, p] = cs * d1
            tabs[hp, 1, p] = sgn * sn * d1
            tabs[hp, 2, p] = cs * d3
            tabs[hp, 3, p] = sgn * sn * d3
            gl[hp, p, 0] = np.exp(lg * 64)
            gl[hp, p, 1] = np.exp(lg)
            gl[hp, p, 2] = np.exp(-lg)
    c["rtab"] = tabs
    c["gam"] = gl
    pw = np.zeros((2, 128, 17), np.float32)
    for hp in range(2):
        for p in range(128):
            win = 2 ** (2 * hp + p // 64 + 1)
            pw[hp, p, 0] = 1.0 / win
            tt = np.arange(16)
            pw[hp, p, 1:] = win / np.minimum(win, tt + 1.0)
    c["poolc"] = pw
    return c


def build_nc(stop_after=None):
    nc = bass.Bass("TRN2", target_bir_lowering=False)
    dram_in = {}

    def din(name, shape, dt=F32):
        dram_in[name] = nc.dram_tensor(name, list(shape), dt, kind="ExternalInput").ap()
        return dram_in[name]

    def dout(name, shape):
        return nc.dram_tensor(name, list(shape), F32, kind="ExternalOutput").ap()

    xin = din("xin", [NT, D])
    st_hgrn = din("st_hgrn", [L, NS, 4, 64, 64])
    st_lru = din("st_lru", [L, NS, BW])
    st_conv = din("st_conv", [L, NS, 3, BW])
    st_ret = din("st_ret", [L, NS, 4, 64, 64])
    st_pool = din("st_pool", [L, NS, 15, BW])
    ffn1_up = din("ffn1_up", [L, D, 2 * DFF])
    ffn1_down = din("ffn1_down", [L, DFF, D])
    w_in = din("w_in", [L, D, 6912])
    w_rgate = din("w_rgate", [L, 4, 64, 64])
    w_igate = din("w_igate", [L, 4, 64, 64])
    w_pool = din("w_pool", [L, 4, 64, 64])
    w_branch = din("w_branch", [L, 4, BW, D])
    w_o = din("w_o", [L, D, D])
    ffn2_up = din("ffn2_up", [L, D, 2 * DFF])
    ffn2_down = din("ffn2_down", [L, DFF, D])
    c_gains_fm = din("gains_fm", [128, 7, 8])
    c_prm_fm = din("prm_fm", [128, L, 12, 2])
    c_cw_fm = din("cw_fm", [128, L, 4, 2])
    c_ident_f = din("ident_f", [128, 128])
    c_ident_b = din("ident_b", [128, 128], BF16)
    c_ones_b = din("ones_b", [128, 128], BF16)
    c_bones_b = din("bones_b", [128, 128], BF16)
    c_cmask = din("cmask", [128, 128])
    c_hmask = din("hmask", [128, 2])
    c_permf = din("permf", [128, 128])
    c_rmask = din("rmask", [128, 512], BF16)
    c_zmask = din("zmask", [128, 16])
    c_rtab = din("rtab", [2, 4, 128, NT])
    c_gam = din("gam", [2, 128, 3])
    c_poolc = din("poolc", [2, 128, 17])

    y_out = dout("y", [NT, D])
    hgrn_p = dout("hgrn_p", [L, 4, 64, 64])
    rglru_p = dout("rglru_p", [L, BW])
    conv_p = dout("conv_p", [L, 3, BW])
    ret_p = dout("ret_p", [L, 4, 64, 64])
    pool_p = dout("pool_p", [L, 15, BW])
    hgrn_s = dout("hgrn_s", [L, NS, 4, 64, 64])
    rglru_s = dout("rglru_s", [L, NS, BW])
    conv_s = dout("conv_s", [L, NS, 3, BW])
    ret_s = dout("ret_s", [L, NS, 4, 64, 64])
    pool_s = dout("pool_s", [L, NS, 15, BW])

    es = ExitStack()
    with es:
        S = Sched(nc)
        nbuf = [0]

        def sbt(es_, shape, dt, name=None):
            nbuf[0] += 1
            return es_.enter_context(nc.sbuf_tensor(f"sb{nbuf[0]}_{name or 't'}", list(shape), dt))

        XT = sbt(es, [128, 8, NT], F32, "XT")
        XN = sbt(es, [128, 8, NT], BF16, "XN")
        bXT = [[Buf() for _ in TILES] for _ in range(8)]
        bXN = [[Buf() for _ in TILES] for _ in range(8)]
        ident_f = sbt(es, [128, 128], F32, "ident_f")
        ident_b = sbt(es, [128, 128], BF16, "ident_b")
        ones_b = sbt(es, [128, 128], BF16, "ones_b")
        bones_b = sbt(es, [128, 128], BF16, "bones_b")
        cmask = sbt(es, [128, 128], F32, "cmask")
        hmask = sbt(es, [128, 2], F32, "hmask")
        permf = sbt(es, [128, 128], F32, "permf")
        rmask = sbt(es, [128, 512], BF16, "rmask")
        zmask = sbt(es, [128, 16], F32, "zmask")
        gam = sbt(es, [128, 2, 3], F32, "gam")
        poolc = sbt(es, [128, 2, 17], F32, "poolc")
        gains = sbt(es, [128, 7, 8], F32, "gains")
        prm = sbt(es, [128, L, 12, 2], F32, "prm")
        cw = sbt(es, [128, L, 4, 2], F32, "cw")
        der = sbt(es, [128, L, 8, 2], F32, "der")
        scr = sbt(es, [128, 8], F32, "scr")
        scr_b = sbt(es, [128, 8], BF16, "scr_b")
        bC = Buf("consts")
        bScr = Buf("scr")
        P_LB0, P_LB1, P_HG, P_CB, P_BR, P_BI, P_LAM, P_RN, P_PS = range(9)

        ps = [es.enter_context(nc.psum_tensor(f"ps{i}", [128, 512], F32)) for i in range(7)]
        psb = es.enter_context(nc.psum_tensor("psb", [128, 1024], BF16))
        bPS = [Buf(f"ps{i}") for i in range(7)]
        bPSB = Buf("psb")

        def dma_in(eng, dst, src, bufs, nonc=False):
            if nonc:
                return S.op(eng, lambda e: e.dma_start(out=dst, in_=src, allow_slow_non_contiguous=True),
                            writes=bufs, dma=True)
            return S.op(eng, lambda e: e.dma_start(out=dst, in_=src), writes=bufs, dma=True)

        dma_in("sp", ident_f[:], c_ident_f, [bC])
        dma_in("sp", ident_b[:], c_ident_b, [bC])
        dma_in("sp", ones_b[:], c_ones_b, [bC])
        dma_in("sp", bones_b[:], c_bones_b, [bC])
        dma_in("sp", cmask[:], c_cmask, [bC])
        dma_in("sp", hmask[:], c_hmask, [bC])
        dma_in("sp", permf[:], c_permf, [bC])
        dma_in("sp", rmask[:], c_rmask, [bC])
        dma_in("sp", zmask[:], c_zmask, [bC])
        dma_in("sp", gam[:], c_gam.rearrange("h p t -> p h t"), [bC])
        dma_in("sp", poolc[:], c_poolc.rearrange("h p t -> p h t"), [bC])
        dma_in("sp", gains[:], c_gains_fm, [bC])
        dma_in("sp", prm[:], c_prm_fm, [bC])
        dma_in("sp", cw[:], c_cw_fm, [bC])
        S.op("dve", lambda e: e.memset(scr[:], 0.0), writes=[bScr])
        S.op("dve", lambda e: e.memset(scr_b[:], 0.0), writes=[bScr])
        bar_scr = {"src": scr[:, 0:1], "act": scr[:, 1:2], "dve": scr[:, 2:3], "pool": scr[:, 3:4],
                   "mm": scr_b[:, 0:8], "ps": ps[6][0:8, 0:8], "reads": [bScr], "pe_writes": [bPS[6]],
                   "w": {"pe": [], "sp": [], "act": [Buf("bar_act")], "dve": [Buf("bar_dve")],
                         "pool": [Buf("bar_pool")]}}

        def barrier():
            S.barrier(bar_scr)

        for l in range(L):
            if l == 0:
                S.op("dve", lambda e: e.memset(der[:, 0, 0, :], 0.0), reads=[bC], writes=[bC])
            else:
                S.op("dve", lambda e: e.tensor_tensor(out=der[:, 1, 0, :], in0=prm[:, 1, P_LB1, :],
                                                      in1=prm[:, 1, P_LB0, :], op=ALU.subtract),
                     reads=[bC], writes=[bC])
                S.op("act", lambda e: e.activation(out=der[:, 1, 0, :], in_=der[:, 1, 0, :], func=AF.Sigmoid),
                     reads=[bC], writes=[bC])
            S.op("dve", lambda e, l=l: e.tensor_scalar(out=der[:, l, 1, :], in0=der[:, l, 0, :], scalar1=-1.0,
                                                       scalar2=1.0, op0=ALU.mult, op1=ALU.add),
                 reads=[bC], writes=[bC])
            S.op("dve", lambda e, l=l: e.tensor_scalar(out=der[:, l, 2, :], in0=der[:, l, 0, :], scalar1=1.0,
                                                       scalar2=-1.0, op0=ALU.mult, op1=ALU.add),
                 reads=[bC], writes=[bC])
            S.op("act", lambda e, l=l: e.activation(out=der[:, l, 3, :], in_=prm[:, l, P_LAM, :], func=AF.Exp,
                                                    scale=-1.0), reads=[bC], writes=[bC])
            S.op("act", lambda e, l=l: e.activation(out=der[:, l, 3, :], in_=der[:, l, 3, :], func=AF.Ln,
                                                    scale=1.0, bias=1.0), reads=[bC], writes=[bC])
            S.op("dve", lambda e, l=l: e.tensor_scalar(out=der[:, l, 3, :], in0=der[:, l, 3, :], scalar1=-8.0,
                                                       scalar2=None, op0=ALU.mult), reads=[bC], writes=[bC])
            S.op("dve", lambda e, l=l: e.tensor_scalar(out=der[:, l, 4, :], in0=der[:, l, 3, :], scalar1=2.0,
                                                       scalar2=None, op0=ALU.mult), reads=[bC], writes=[bC])

        def fsz(ap):
            n = 1
            for d in ap.shape[1:]:
                n *= d
            return n

        def mm(out, lhsT, rhs, start, stop, reads, writes, tp=None):
            c = 0.1 + fsz(rhs) / 1700.0
            if tp is None:
                S.op("pe", lambda e: e.matmul(out, lhsT=lhsT, rhs=rhs, start=start, stop=stop),
                     reads=reads, writes=writes, cost=c)
            else:
                S.op("pe", lambda e: e.matmul(out, lhsT=lhsT, rhs=rhs, start=start, stop=stop, tile_position=tp),
                     reads=reads, writes=writes, cost=c)

        def act(out, in_, func, reads, writes, scale=None, bias=None):
            kw = {}
            if scale is not None:
                kw["scale"] = scale
            if bias is not None:
                kw["bias"] = bias
            tag = {AF.Sigmoid: "sig", AF.Silu: "silu", AF.Ln: "lnexp", AF.Exp: "lnexp"}.get(func)
            S.op("act", lambda e: e.activation(out=out, in_=in_, func=func, **kw), reads=reads, writes=writes,
                 cost=0.25 + fsz(out) / 1100.0, tag=tag)

        def sigm(out, in_, reads, writes):
            act(out, in_, AF.Exp, reads, writes, scale=-1.0)
            act(out, out, AF.Ln, writes, writes, scale=1.0, bias=1.0)
            act(out, out, AF.Exp, writes, writes, scale=-1.0)

        def ecost(eng, out):
            if eng == "pool":
                return 0.3 + fsz(out) / 450.0
            if eng == "act":
                return 0.25 + fsz(out) / 1100.0
            return 0.15 + fsz(out) / 900.0

        def tt(eng, out, in0, in1, op, reads, writes):
            S.op(eng, lambda e: e.tensor_tensor(out=out, in0=in0, in1=in1, op=op), reads=reads, writes=writes,
                 cost=ecost(eng, out))

        def ts(eng, out, in0, s1, s2, op0, op1, reads, writes):
            if op1 is None:
                S.op(eng, lambda e: e.tensor_scalar(out=out, in0=in0, scalar1=s1, scalar2=None, op0=op0),
                     reads=reads, writes=writes, cost=ecost(eng, out))
            else:
                S.op(eng, lambda e: e.tensor_scalar(out=out, in0=in0, scalar1=s1, scalar2=s2, op0=op0, op1=op1),
                     reads=reads, writes=writes, cost=ecost(eng, out))

        def stt(eng, out, in0, scalar, in1, op0, op1, reads, writes):
            S.op(eng, lambda e: e.scalar_tensor_tensor(out=out, in0=in0, scalar=scalar, in1=in1, op0=op0, op1=op1),
                 reads=reads, writes=writes, cost=ecost(eng, out))

        def cp(eng, out, in_, reads, writes):
            if eng == "act":
                S.op("act", lambda e: e.activation(out=out, in_=in_, func=AF.Copy), reads=reads, writes=writes,
                     cost=ecost("act", out))
            else:
                S.op(eng, lambda e: e.tensor_copy(out=out, in_=in_), reads=reads, writes=writes,
                     cost=ecost(eng, out))

        def dma(eng, out, in_, reads=(), writes=(), nonc=False):
            if nonc:
                return S.op(eng, lambda e: e.dma_start(out=out, in_=in_, allow_slow_non_contiguous=True),
                            reads=reads, writes=writes, dma=True)
            return S.op(eng, lambda e: e.dma_start(out=out, in_=in_), reads=reads, writes=writes, dma=True)

        def tr(out, in_, ident, reads, writes):
            S.op("pe", lambda e: e.transpose(out=out, in_=in_, identity=ident), reads=reads, writes=writes, cost=0.15)

        def scan(out, d0, d1, init, reads, writes):
            S.op("dve", lambda e: e.tensor_tensor_scan(out=out, data0=d0, data1=d1, initial=init, op0=ALU.mult,
                                                       op1=ALU.add), reads=reads, writes=writes,
                 cost=0.2 + fsz(out) / 450.0)

        def memset(ap, val, writes):
            S.op("dve", lambda e: e.memset(ap, val), writes=writes)

        def reduce_add(out, in_, reads, writes):
            S.op("dve", lambda e: e.tensor_reduce(out=out, in_=in_, axis=AX.X, op=ALU.add), reads=reads,
                 writes=writes)

        def wload(dst, src, buf, nobar=False):
            return S.op("pool", lambda e: e.dma_start(out=dst, in_=src), writes=[buf], dma=True, nobar=nobar)

        with ExitStack() as pes:
            NSTG = 4
            stg = [sbt(pes, [128, D], F32) for _ in range(NSTG)]
            bstg = [Buf() for _ in range(NSTG)]
            nblk = 17

            def in_region():
              for blk in range(nblk):
                r0 = blk * 128
                n = 128 if blk < 16 else NS
                ti = min(blk // 4, 4)
                st_, bs_ = stg[blk % NSTG], bstg[blk % NSTG]
                dma_in("sp", st_[0:n, :], xin[r0:r0 + n, :], [bs_])
                for g in range(2):
                    pb, bpb = ps[(blk * 2 + g) % 4], bPS[(blk * 2 + g) % 4]
                    for q in range(4):
                        kc = g * 4 + q
                        tr(pb[:, q * 128:q * 128 + n], st_[0:n, kc * 128:(kc + 1) * 128], ident_f[0:n, 0:n], [bs_, bC],
                           [bpb])
                    src = pb[:].rearrange("p (q t) -> p q t", t=128)[:, :, 0:n]
                    cp("act" if g == 0 else "dve", XT[:, g * 4:g * 4 + 4, r0:r0 + n], src, [bpb],
                       [bXT[kc][ti] for kc in range(g * 4, g * 4 + 4)])

            S.schedule(S.capture(in_region))
        barrier()

        def rmsnorm(pes, gi):
            sq = [sbt(pes, [128, 8, 512], BF16) for _ in range(2)]
            bsq = [Buf() for _ in range(2)]
            rs = [sbt(pes, [128, 512], F32) for _ in range(2)]
            brs = [Buf() for _ in range(2)]

            def square(ti):
                c0, c1 = TILES[ti]
                act(sq[ti % 2][:, :, 0:c1 - c0], XT[:, :, c0:c1], AF.Square, [bXT[k][ti] for k in range(8)],
                    [bsq[ti % 2]])

            square(0)
            for ti, (c0, c1) in enumerate(TILES):
                n = c1 - c0
                s_, bs_ = sq[ti % 2], bsq[ti % 2]
                r_, br_ = rs[ti % 2], brs[ti % 2]
                pb, bpb = ps[4 + ti % 2], bPS[4 + ti % 2]
                for kc in range(8):
                    mm(pb[:, 0:n], ones_b[:], s_[:, kc, 0:n], kc == 0, kc == 7, [bs_, bC], [bpb])
                if ti + 1 < len(TILES):
                    square(ti + 1)
                act(r_[:, 0:n], pb[:, 0:n], AF.Ln, [bpb], [br_], scale=1.0 / D, bias=EPS)
                act(r_[:, 0:n], r_[:, 0:n], AF.Exp, [br_], [br_], scale=-0.5)
                for kc in range(8):
                    stt("dve", XN[:, kc, c0:c1], XT[:, kc, c0:c1], gains[:, gi, kc:kc + 1], r_[:, 0:n],
                        ALU.mult, ALU.mult, [bXT[kc][ti], br_, bC], [bXN[kc][ti]])

        def ffn(w_up, w_down, gi):
            with ExitStack() as pes:
                rmsnorm(pes, gi)
                NR = 2
                wa = [sbt(pes, [128, 8, 512], BF16) for _ in range(NR)]
                wb = [sbt(pes, [128, 8, 512], BF16) for _ in range(NR)]
                wd = [sbt(pes, [128, 4, D], BF16) for _ in range(NR)]
                bwa = [Buf() for _ in range(NR)]
                bwb = [Buf() for _ in range(NR)]
                bwd = [Buf() for _ in range(NR)]
                hg = sbt(pes, [128, 8, NT], BF16)
                bhg = [[Buf() for _ in TILES] for _ in range(8)]
                sa = [sbt(pes, [128, 512], F32) for _ in range(3)]
                bsa = [Buf() for _ in range(3)]
                cnt = 0
                ocnt = 0
                for g in range(6):
                    wdt = 512 if g < 5 else 256
                    nfc = wdt // 128
                    r = g % NR
                    hoff = 4 * (g % 2)
                    wload(wa[r][:, :, 0:wdt], w_up[:, 512 * g:512 * g + wdt].rearrange("(kc p) c -> p kc c", p=128),
                          bwa[r])
                    wload(wb[r][:, :, 0:wdt],
                          w_up[:, DFF + 512 * g:DFF + 512 * g + wdt].rearrange("(kc p) c -> p kc c", p=128), bwb[r])
                    wload(wd[r][:, 0:nfc, :], w_down[512 * g:512 * g + wdt, :].rearrange("(fc p) c -> p fc c", p=128),
                          bwd[r])
                    for ti, (c0, c1) in enumerate(TILES):
                        n = c1 - c0
                        for fc in range(nfc):
                            pa, bpa = ps[(0, 1, 4)[cnt % 3]], bPS[(0, 1, 4)[cnt % 3]]
                            pb, bpb = ps[(2, 3, 5)[cnt % 3]], bPS[(2, 3, 5)[cnt % 3]]
                            s_, bs_ = sa[cnt % 3], bsa[cnt % 3]
                            cnt += 1
                            for kc in range(8):
                                mm(pa[:, 0:n], wa[r][:, kc, fc * 128:(fc + 1) * 128], XN[:, kc, c0:c1], kc == 0,
                                   kc == 7, [bwa[r], bXN[kc][ti]], [bpa])
                            for kc in range(8):
                                mm(pb[:, 0:n], wb[r][:, kc, fc * 128:(fc + 1) * 128], XN[:, kc, c0:c1], kc == 0,
                                   kc == 7, [bwb[r], bXN[kc][ti]], [bpb])
                            act(s_[:, 0:n], pa[:, 0:n], AF.Silu, [bpa], [bs_])
                            tt("dve", hg[:, hoff + fc, c0:c1], s_[:, 0:n], pb[:, 0:n], ALU.mult, [bs_, bpb],
                               [bhg[hoff + fc][ti]])
                    if g % 2 == 0:
                        continue
                    dsrc = [((g - 1) % NR, fc, fc) for fc in range(4)] + [(r, fc, 4 + fc) for fc in range(nfc)]
                    for ti, (c0, c1) in enumerate(TILES):
                        n = c1 - c0
                        for dc in range(8):
                            po, bpo = ps[4 + ocnt % 3], bPS[4 + ocnt % 3]
                            ocnt += 1
                            for i_, (rr, fw, fh) in enumerate(dsrc):
                                mm(po[:, 0:n], wd[rr][:, fw, dc * 128:(dc + 1) * 128], hg[:, fh, c0:c1], i_ == 0,
                                   i_ == len(dsrc) - 1, [bwd[rr], bhg[fh][ti]], [bpo])
                            stt("dve", XT[:, dc, c0:c1], po[:, 0:n], 0.5, XT[:, dc, c0:c1], ALU.mult, ALU.add,
                                [bpo, bXT[dc][ti]], [bXT[dc][ti]])
            barrier()

        def load_wchunk(pes_ring, l, col0):
            t_, b_ = pes_ring
            wload(t_[:], w_in[l][:, col0:col0 + 128].rearrange("(kc p) c -> p kc c", p=128), b_, nobar=True)

        def proj(wt, bw, ti, pbank, bpb):
            c0, c1 = TILES[ti]
            n = c1 - c0
            for kc in range(8):
                mm(pbank[:, 0:n], wt[:, kc, :], XN[:, kc, c0:c1], kc == 0, kc == 7, [bw, bXN[kc][ti]], [bpb])

        def to_tokmajor_bf(src_bf, bsrc, n, dstT, bdst):
            nb = (n + 127) // 128
            for q in range(nb):
                m = min(128, n - q * 128)
                tr(psb[0:m, q * 128:(q + 1) * 128], src_bf[:, q * 128:q * 128 + m], ident_b[:], [bsrc, bC], [bPSB])
            m = min(128, n)
            cp("act", dstT[0:m, 0:nb, :], psb[0:m, 0:nb * 128].rearrange("p (q c) -> p q c", c=128), [bPSB], [bdst])

        class LA:
            pass

        def la_alloc(pes, need_od=True):
            w = LA()
            w.qd = sbt(pes, [128, 512], BF16)
            w.kt = sbt(pes, [128, 512], BF16)
            w.kd = sbt(pes, [128, 512], BF16)
            w.vb = sbt(pes, [128, 512], BF16)
            w.kdT = sbt(pes, [128, 4, 128], BF16)
            w.vT = sbt(pes, [128, 4, 128], BF16)
            w.PV = sbt(pes, [128, 2048], BF16)
            w.P = w.PV[:, 0:1024].rearrange("p (h t) -> p h t", h=2)
            w.Sf = sbt(pes, [128, 64], F32)
            w.Sb = sbt(pes, [128, 2, 64], BF16)
            w.qd2 = sbt(pes, [128, 2, 512], BF16)
            w.kx = sbt(pes, [128, 512], BF16)
            w.ep = sbt(pes, [128, 4], F32)
            w.sqb2 = sbt(pes, [128, 2, NS], BF16)
            w.sg = sbt(pes, [128, 512], F32)
            w.sq = sbt(pes, [128, 512], F32)
            w.el = sbt(pes, [128, 8], F32)
            w.osq = sbt(pes, [128, 512], BF16)
            w.rst = sbt(pes, [128, 512], F32)
            w.od = sbt(pes, [128, 512], F32) if need_od else None
            w.S0 = sbt(pes, [128, NS, 64], F32)
            w.Snb = sbt(pes, [128, NS, 64], BF16)
            w.vbd = w.PV[0:NS, :].rearrange("p (h b v) -> p h b v", h=2, b=NS)
            w.sqb = sbt(pes, [128, NS], BF16)
            for nm in ["qd", "kt", "kd", "vb", "kdT", "vT", "P", "Sf", "Sb", "sg", "sq", "el", "osq", "rst", "od",
                       "S0", "Snb", "vbd", "sqb", "qd2", "kx", "ep", "sqb2"]:
                setattr(w, "b_" + nm, Buf(nm))
            w.Sn = w.S0
            w.b_Sn = w.b_S0
            w.b_vbd = w.b_P
            return w

        PS_S, PS_U, PS_O, PS_N = 4, 5, 6, 4

        def la_alloc2(pes, need_od=True):
            w0 = la_alloc(pes, need_od)
            w1 = LA()
            w1.__dict__.update(w0.__dict__)
            for nm, shape, dt in [("qd", [128, 512], BF16), ("qd2", [128, 2, 512], BF16), ("kt", [128, 512], BF16),
                                  ("kx", [128, 512], BF16), ("kdT", [128, 4, 128], BF16), ("vT", [128, 4, 128], BF16),
                                  ("ep", [128, 4], F32), ("sg", [128, 512], F32)]:
                setattr(w1, nm, sbt(pes, shape, dt))
                setattr(w1, "b_" + nm, Buf(nm))
            return (w0, w1)

        def la_prep(w, qf, bqf, ktf, bktf, kdf, bkdf, el, bel):
            v4 = lambda ap: ap.rearrange("p (q r j) -> p q r j", q=4, r=2)
            el_e = el[:, 0:8].rearrange("p (q r) -> p q r", r=2)[:, :, 0:1].to_broadcast([128, 4, 64])
            el_o = el[:, 0:8].rearrange("p (q r) -> p q r", r=2)[:, :, 1:2].to_broadcast([128, 4, 64])
            tt("dve", w.qd2[:], qf.unsqueeze(1).to_broadcast([128, 2, 512]),
               hmask[:].unsqueeze(2).to_broadcast([128, 2, 512]), ALU.mult, [bqf, bC], [w.b_qd2])
            cp("act", v4(w.qd[:])[:, :, 0, :], v4(qf)[:, :, 0, :], [bqf], [w.b_qd])
            tt("dve", v4(w.qd[:])[:, :, 1, :], v4(qf)[:, :, 1, :], el_e, ALU.mult, [bqf, bel], [w.b_qd])
            cp("act", w.kt[:], ktf, [bktf], [w.b_kt])
            cp("act", v4(w.kx[:])[:, :, 0, :], v4(kdf)[:, :, 0, :], [bkdf], [w.b_kx])
            cp("act", v4(w.kx[:])[:, :, 1, :], v4(ktf)[:, :, 1, :], [bktf], [w.b_kx])
            tt("dve", v4(w.kd[:])[:, :, 0, :], v4(kdf)[:, :, 0, :], el_o, ALU.mult, [bkdf, bel], [w.b_kd])
            cp("act", v4(w.kd[:])[:, :, 1, :], v4(kdf)[:, :, 1, :], [bkdf], [w.b_kd])
            tt("dve", w.ep[:], el[:, 0:8].rearrange("p (q r) -> p q r", r=2)[:, :, 0],
               el[:, 0:8].rearrange("p (q r) -> p q r", r=2)[:, :, 1], ALU.mult, [bel], [w.b_ep])

        def la_prompt_tile(w, ti, first):
            pS, bpS = ps[PS_S], bPS[PS_S]
            pU, pO = ps[PS_U], ps[PS_O]
            pU3 = pU[:, 0:256].rearrange("p (c v) -> p c v", v=64)

            def scores(hh):
                for pr in range(4):
                    b0 = pr * 128
                    mm(pS[:, b0:b0 + 64], w.kt[:, b0:b0 + 128], w.qd2[:, hh, b0:b0 + 64], True, True,
                       [w.b_kt, w.b_qd2], [bpS])
                    mm(pS[:, b0 + 64:b0 + 128], w.kx[:, b0:b0 + 128], w.qd2[:, hh, b0 + 64:b0 + 128], True, True,
                       [w.b_kx, w.b_qd2], [bpS])
                tt("dve", w.P[:, hh, :].rearrange("p (q t) -> p q t", t=128),
                   pS[:].rearrange("p (q t) -> p q t", t=128),
                   cmask[:].unsqueeze(1).to_broadcast([128, 4, 128]), ALU.mult, [bpS, bC], [w.b_P])

            scores(0)
            for pr in range(4):
                for hh in range(2):
                    mm(pU3[64 * hh:64 * hh + 64, pr, :], w.kdT[:, pr, 64 * hh:64 * hh + 64],
                       w.vT[:, pr, 64 * hh:64 * hh + 64], True, True, [w.b_kdT, w.b_vT], [bPS[PS_U]],
                       tp=(0, 64 * hh))
            scores(1)
            for pr in range(4):
                b0 = pr * 128
                f0 = first and pr == 0
                for hh in range(2):
                    o_ap = pO[64 * hh:64 * hh + 64, b0:b0 + 128]
                    mm(o_ap, w.vT[:, pr, 64 * hh:64 * hh + 64], w.P[:, hh, b0:b0 + 128], True, f0,
                       [w.b_vT, w.b_P], [bPS[PS_O]], tp=(0, 64 * hh))
                    if not f0:
                        mm(o_ap, w.Sb[:, hh, :], w.qd[:, b0:b0 + 128], False, True, [w.b_Sb, w.b_qd], [bPS[PS_O]],
                           tp=(0, 64 * hh))
                if f0:
                    cp("dve", w.Sf[:], pU3[:, pr, :], [bPS[PS_U]], [w.b_Sf])
                else:
                    stt("dve", w.Sf[:], w.Sf[:], w.ep[:, pr:pr + 1], pU3[:, pr, :], ALU.mult, ALU.add,
                        [w.b_Sf, bPS[PS_U], w.b_ep], [w.b_Sf])
                tt("dve", w.Sb[:], w.Sf[:].unsqueeze(1).to_broadcast([128, 2, 64]),
                   hmask[:].unsqueeze(2).to_broadcast([128, 2, 64]), ALU.mult, [w.b_Sf, bC], [w.b_Sb])

        def la_pipeline(regs):
            def region():
                for (X, Y, post, pre) in regs:
                    pre()
                    for t in range(4):
                        X(t)
                        Y(t)
                    X(4)
                    post()
            S.schedule(S.capture(region))

        def la_sample(w, st_in, st_out, l, hp, sq_ap, e1_fn):
            pA, pB, pO = ps[PS_S], ps[PS_U], ps[PS_O]
            dma("sp", w.S0[:], st_in[l, :, 2 * hp:2 * hp + 2, :, :].rearrange("b h k v -> (h k) b v"),
                writes=[w.b_S0])
            e1_fn(w)
            vsrc = w.vT[0:NS, 0, :].rearrange("p (h v) -> p h v", h=2).unsqueeze(2).to_broadcast([NS, 2, NS, 64])
            isrc = ident_f[0:NS, 0:NS].unsqueeze(1).unsqueeze(3).to_broadcast([NS, 2, NS, 64])
            tt("dve", w.vbd[:], vsrc, isrc, ALU.mult, [w.b_vT, bC], [w.b_vbd])
            for hh in range(2):
                for jb, pbank in enumerate((pA, pB)):
                    mm(pbank[64 * hh:64 * hh + 64, :], w.kdT[0:NS, 0, 64 * hh:64 * hh + 64],
                       w.vbd[:, hh, 8 * jb:8 * jb + 8, :].rearrange("p b v -> p (b v)"), True, True,
                       [w.b_kdT, w.b_vbd], [bPS[PS_S] if jb == 0 else bPS[PS_U]], tp=(0, 64 * hh))
            for jb, (pbank, bpb) in enumerate(((pA, bPS[PS_S]), (pB, bPS[PS_U]))):
                tt("dve", w.Sn[:, 8 * jb:8 * jb + 8, :], w.Sn[:, 8 * jb:8 * jb + 8, :],
                   pbank[:].rearrange("p (b v) -> p b v", v=64), ALU.add, [w.b_Sn, bpb], [w.b_Sn])
            dma("sp", st_out[l, :, 2 * hp:2 * hp + 2, :, :].rearrange("b h k v -> (h k) b v"), w.Sn[:],
                reads=[w.b_Sn])
            cp("act", w.Snb[:], w.Sn[:], [w.b_Sn], [w.b_Snb])
            tt("dve", w.sqb2[:], sq_ap.unsqueeze(1).to_broadcast([128, 2, NS]),
               hmask[:].unsqueeze(2).to_broadcast([128, 2, NS]), ALU.mult, [w.b_sq, bC], [w.b_sqb2])
            for b in range(NS):
                for hh in range(2):
                    mm(pO[64 * hh:64 * hh + 64, b:b + 1], w.Snb[:, b, :],
                       w.sqb2[:, hh, b:b + 1], True, True, [w.b_Snb, w.b_sqb2], [bPS[PS_O]], tp=(0, 64 * hh))

        def la_finish(w, n, gcol, dst, bdst, group_norm):
            pO, pN = ps[PS_O], ps[PS_N]
            if group_norm:
                cp("act", w.od[:, 0:n], pO[:, 0:n], [bPS[PS_O]], [w.b_od])
                cp("dve", w.osq[:, 0:n], w.od[:, 0:n], [w.b_od], [w.b_osq])
                mm(pN[:, 0:n], bones_b[:], w.osq[:, 0:n], True, True, [w.b_osq, bC], [bPS[PS_N]])
                stt("dve", w.od[:, 0:n], pN[:, 0:n], -1.0 / 64, w.od[:, 0:n], ALU.mult, ALU.add,
                    [bPS[PS_N], w.b_od], [w.b_od])
                src, bsrc = w.od[:, 0:n], w.b_od
            else:
                src, bsrc = pO[:, 0:n], bPS[PS_O]
            act(w.osq[:, 0:n], src, AF.Square, [bsrc], [w.b_osq])
            mm(pN[:, 0:n], bones_b[:], w.osq[:, 0:n], True, True, [w.b_osq, bC], [bPS[PS_N]])
            act(w.rst[:, 0:n], pN[:, 0:n], AF.Ln, [bPS[PS_N]], [w.b_rst], scale=1.0 / 64, bias=EPS)
            act(w.rst[:, 0:n], w.rst[:, 0:n], AF.Exp, [w.b_rst], [w.b_rst], scale=-0.5)
            stt("dve", w.rst[:, 0:n], src, gcol, w.rst[:, 0:n], ALU.mult, ALU.mult, [bsrc, w.b_rst, bC], [w.b_rst])
            tt("dve", dst, w.rst[:, 0:n], w.sg[:, 0:n], ALU.mult, [w.b_rst, w.b_sg], [bdst])

        def mixer(l):
            with ExitStack() as pes:
                OT = sbt(pes, [128, 8, NT], BF16)
                bOT = [[Buf() for _ in TILES] for _ in range(8)]
                ring = [(sbt(pes, [128, 8, 128], BF16), Buf()) for _ in range(8)]
                rcnt = [0]

                def nextw(col0):
                    r_ = ring[rcnt[0] % 8]
                    rcnt[0] += 1
                    load_wchunk(r_, l, col0)
                    return r_

                preA = [nextw(s_ * 256) for s_ in range(4)]
                rmsnorm_es = ExitStack()
                with rmsnorm_es:
                    rmsnorm(rmsnorm_es, 3 * l + 1)
                    barrier()

                with ExitStack() as bes:
                    ws = la_alloc2(bes, need_od=True)
                    Tp = [[sbt(bes, [128, 512], F32) for _ in range(5)] for _ in range(2)]
                    pass
                    bTp = [[Buf() for _ in range(5)] for _ in range(2)]
                    regsA = []
                    wh = {}
                    for hp in range(2):
                        if hp == 0:
                            wh[("A", 0)] = preA

                        def preA_fn(hp=hp):
                            if hp == 1:
                                wh[("A", 1)] = [nextw(s_ * 256 + 128) for s_ in range(4)]
                        lb = der[:, l, 0, hp:hp + 1]
                        oml = der[:, l, 1, hp:hp + 1]
                        noml = der[:, l, 2, hp:hp + 1]

                        def XA(ti, hp=hp, lb=lb, oml=oml, noml=noml):
                            wq, wf, wi, wg = wh[("A", hp)]
                            w = ws[(ti + hp) % 2]
                            T, bT = Tp[(ti + hp) % 2], bTp[(ti + hp) % 2]
                            c0, c1 = TILES[ti]
                            n = c1 - c0
                            Lc = 64 if ti < 4 else 1
                            proj(wf[0], wf[1], ti, ps[0], bPS[0])
                            proj(wq[0], wq[1], ti, ps[1], bPS[1])
                            proj(wi[0], wi[1], ti, ps[2], bPS[2])
                            proj(wg[0], wg[1], ti, ps[3], bPS[3])
                            sigm(T[0][:, 0:n], ps[0][:, 0:n], [bPS[0]], [bT[0]])
                            sigm(w.sq[:, 0:n], ps[1][:, 0:n], [bPS[1]], [w.b_sq])
                            sigm(w.sg[:, 0:n], ps[3][:, 0:n], [bPS[3]], [w.b_sg])
                            cp("act", w.vb[:, 0:n], ps[2][:, 0:n], [bPS[2]], [w.b_vb])
                            tt("dve", w.sq[:, 0:n], w.sq[:, 0:n], ps[1][:, 0:n], ALU.mult, [w.b_sq, bPS[1]], [w.b_sq])
                            tt("dve", w.sg[:, 0:n], w.sg[:, 0:n], ps[3][:, 0:n], ALU.mult, [w.b_sg, bPS[3]], [w.b_sg])
                            ts("dve", T[1][:, 0:n], T[0][:, 0:n], oml, lb, ALU.mult, ALU.add, [bT[0], bC], [bT[1]])
                            act(T[1][:, 0:n], T[1][:, 0:n], AF.Ln, [bT[1]], [bT[1]], scale=1.0, bias=1e-30)
                            msk = rmask[:, 0:n] if ti < 4 else zmask[:, 0:n]
                            scan(T[2][:, 0:n], msk, T[1][:, 0:n], 0.0, [bT[1], bC], [bT[2]])
                            ts("dve", T[0][:, 0:n], T[0][:, 0:n], noml, oml, ALU.mult, ALU.add, [bT[0], bC], [bT[0]])
                            ts("dve", T[3][:, 0:n], T[2][:, 0:n], -1.0, 80.0, ALU.mult, ALU.min, [bT[2]], [bT[3]])
                            c3 = T[2][:, 0:n].rearrange("p (c j) -> p c j", j=Lc)
                            tt("dve", T[4][:, 0:n].rearrange("p (c j) -> p c j", j=Lc), c3,
                               c3[:, :, Lc - 1:Lc].to_broadcast([128, n // Lc, Lc]), ALU.subtract, [bT[2]], [bT[4]])
                            act(T[2][:, 0:n], T[2][:, 0:n], AF.Exp, [bT[2]], [bT[2]])
                            act(T[3][:, 0:n], T[3][:, 0:n], AF.Exp, [bT[3]], [bT[3]])
                            act(T[4][:, 0:n], T[4][:, 0:n], AF.Exp, [bT[4]], [bT[4]], scale=-1.0)
                            if ti < 4:
                                cp("act", w.el[:, 0:8], T[2][:, 0:n].rearrange("p (c j) -> p c j", j=64)[:, :, 63],
                                   [bT[2]], [w.b_el])
                                tt("dve", T[1][:, 0:n], w.sq[:, 0:n], T[2][:, 0:n], ALU.mult, [w.b_sq, bT[2]], [bT[1]])
                                tt("dve", T[3][:, 0:n], T[0][:, 0:n], T[3][:, 0:n], ALU.mult, [bT[0], bT[3]], [bT[3]])
                                tt("dve", T[4][:, 0:n], T[0][:, 0:n], T[4][:, 0:n], ALU.mult, [bT[0], bT[4]], [bT[4]])
                                la_prep(w, T[1][:, 0:n], bT[1], T[3][:, 0:n], bT[3], T[4][:, 0:n], bT[4], w.el, w.b_el)
                            else:
                                tt("dve", w.kd[:, 0:n], T[0][:, 0:n], T[4][:, 0:n], ALU.mult, [bT[0], bT[4]], [w.b_kd])
                            to_tokmajor_bf(w.kd, w.b_kd, n, w.kdT, w.b_kdT)
                            to_tokmajor_bf(w.vb, w.b_vb, n, w.vT, w.b_vT)

                        def YA(ti, hp=hp):
                            w = ws[(ti + hp) % 2]
                            c0, c1 = TILES[ti]
                            la_prompt_tile(w, ti, ti == 0)
                            if ti == 3:
                                dma("sp", hgrn_p[l, 2 * hp:2 * hp + 2, :, :].rearrange("h k v -> (h k) v"),
                                    w.Sf[:], reads=[w.b_Sf])
                            la_finish(w, c1 - c0, prm[:, l, P_HG, hp:hp + 1], OT[:, hp, c0:c1], bOT[hp][ti], False)

                        def postA(hp=hp):
                            w = ws[hp % 2]
                            T, bT = Tp[hp % 2], bTp[hp % 2]
                            c0, c1 = TILES[4]
                            n = c1 - c0

                            def e1(w_, n=n):
                                tt("dve", w_.Sn[:], w_.S0[:], T[2][:, 0:n].unsqueeze(2).to_broadcast([128, NS, 64]),
                                   ALU.mult, [w_.b_S0, bT[2]], [w_.b_Sn])
                            la_sample(w, st_hgrn, hgrn_s, l, hp, w.sq[:, 0:n], e1)
                            la_finish(w, n, prm[:, l, P_HG, hp:hp + 1], OT[:, hp, c0:c1], bOT[hp][4], False)

                        regsA.append((XA, YA, postA, preA_fn))
                    QKp = [[Tp[0][0], Tp[0][1]], [Tp[1][0], Tp[1][1]]]
                    bQKp = [[bTp[0][0], bTp[0][1]], [bTp[1][0], bTp[1][1]]]
                    QS, KS = Tp[0][2], Tp[1][2]
                    bQS, bKS = bTp[0][2], bTp[1][2]
                    tb = [Tp[0][3], Tp[0][4], Tp[1][3], Tp[1][4]]
                    btb = [bTp[0][3], bTp[0][4], bTp[1][3], bTp[1][4]]
                    regsC = []
                    pass
                    for hp in range(2):
                        def preC_fn(hp=hp):
                            wh[("C", hp)] = [nextw(1536 + s_ * 256 + 128 * hp) for s_ in range(4)]

                        def XC(ti, hp=hp):
                            wq, wk, wv, wg = wh[("C", hp)]
                            w = ws[(ti + hp) % 2]
                            Q, K = QKp[(ti + hp) % 2]
                            bQ, bK = bQKp[(ti + hp) % 2]
                            c0, c1 = TILES[ti]
                            n = c1 - c0
                            for q in range(4):
                                dma_in("sp", tb[q][:, 0:n], c_rtab[hp, q, :, c0:c1], [btb[q]])
                            proj(wq[0], wq[1], ti, ps[0], bPS[0])
                            proj(wk[0], wk[1], ti, ps[1], bPS[1])
                            proj(wv[0], wv[1], ti, ps[2], bPS[2])
                            proj(wg[0], wg[1], ti, ps[3], bPS[3])
                            sigm(w.sg[:, 0:n], ps[3][:, 0:n], [bPS[3]], [w.b_sg])
                            cp("act", w.vb[:, 0:n], ps[2][:, 0:n], [bPS[2]], [w.b_vb])
                            cp("act", Q[:, 0:n], ps[0][:, 0:n], [bPS[0]], [bQ])
                            cp("act", K[:, 0:n], ps[1][:, 0:n], [bPS[1]], [bK])
                            tt("dve", w.sg[:, 0:n], w.sg[:, 0:n], ps[3][:, 0:n], ALU.mult, [w.b_sg, bPS[3]], [w.b_sg])
                            mm(ps[0][:, 0:n], permf[:], Q[:, 0:n], True, True, [bQ, bC], [bPS[0]])
                            mm(ps[1][:, 0:n], permf[:], K[:, 0:n], True, True, [bK, bC], [bPS[1]])
                            tt("dve", Q[:, 0:n], Q[:, 0:n], tb[0][:, 0:n], ALU.mult, [bQ, btb[0]], [bQ])
                            tt("dve", QS[:, 0:n], ps[0][:, 0:n], tb[1][:, 0:n], ALU.mult, [bPS[0], btb[1]], [bQS])
                            tt("dve", Q[:, 0:n], Q[:, 0:n], QS[:, 0:n], ALU.add, [bQ, bQS], [bQ])
                            tt("dve", K[:, 0:n], K[:, 0:n], tb[2][:, 0:n], ALU.mult, [bK, btb[2]], [bK])
                            tt("dve", KS[:, 0:n], ps[1][:, 0:n], tb[3][:, 0:n], ALU.mult, [bPS[1], btb[3]], [bKS])
                            tt("dve", K[:, 0:n], K[:, 0:n], KS[:, 0:n], ALU.add, [bK, bKS], [bK])
                            if ti < 4:
                                ts("dve", KS[:, 0:n], K[:, 0:n], gam[:, hp, 0:1], None, ALU.mult, None, [bK, bC], [bKS])
                                if ti == 0:
                                    memset(w.el[:], 1.0, [w.b_el])
                                    ts("dve", w.el[:], w.el[:], gam[:, hp, 0:1], None, ALU.mult, None, [w.b_el, bC],
                                       [w.b_el])
                                la_prep(w, Q[:, 0:n], bQ, K[:, 0:n], bK, KS[:, 0:n], bKS, w.el, w.b_el)
                            else:
                                ts("dve", w.kd[:, 0:n], K[:, 0:n], gam[:, hp, 1:2], None, ALU.mult, None, [bK, bC],
                                   [w.b_kd])
                                ts("dve", w.sq[:, 0:n], Q[:, 0:n], gam[:, hp, 2:3], None, ALU.mult, None,
                                   [bQ, bC], [w.b_sq])
                            to_tokmajor_bf(w.kd, w.b_kd, n, w.kdT, w.b_kdT)
                            to_tokmajor_bf(w.vb, w.b_vb, n, w.vT, w.b_vT)

                        def YC(ti, hp=hp):
                            w = ws[(ti + hp) % 2]
                            c0, c1 = TILES[ti]
                            la_prompt_tile(w, ti, ti == 0)
                            if ti == 3:
                                dma("sp", ret_p[l, 2 * hp:2 * hp + 2, :, :].rearrange("h k v -> (h k) v"),
                                    w.Sf[:], reads=[w.b_Sf])
                            la_finish(w, c1 - c0, prm[:, l, P_RN, hp:hp + 1], OT[:, 4 + hp, c0:c1], bOT[4 + hp][ti],
                                      True)

                        def postC(hp=hp):
                            w = ws[hp % 2]
                            c0, c1 = TILES[4]
                            n = c1 - c0

                            def e1(w_, hp=hp):
                                ts("dve", w_.Sn[:], w_.S0[:], gam[:, hp, 1:2], None, ALU.mult, None,
                                   [w_.b_S0, bC], [w_.b_Sn])
                            la_sample(w, st_ret, ret_s, l, hp, w.sq[:, 0:n], e1)
                            la_finish(w, n, prm[:, l, P_RN, hp:hp + 1], OT[:, 4 + hp, c0:c1], bOT[4 + hp][4], True)

                        regsC.append((XC, YC, postC, preC_fn))
                    la_pipeline(regsA + regsC)
                    preB = [nextw(1024), nextw(1280)]
                    preD = [nextw(2560)]
                barrier()
                if stop_after == ("mixC", l):
                    return None

                def make_B(bes):
                    UX = sbt(bes, [128, 3 + 512], F32)
                    bUX = Buf()
                    Tp = [[sbt(bes, [128, 512], F32) for _ in range(6)] for _ in range(2)]
                    bTp = [[Buf() for _ in range(6)] for _ in range(2)]
                    xcbp = [sbt(bes, [128, 512], BF16) for _ in range(2)]
                    bxcbp = [Buf() for _ in range(2)]
                    wr = sbt(bes, [128, 128], BF16)
                    wi_ = sbt(bes, [128, 128], BF16)
                    bwr = Buf()
                    hprev = sbt(bes, [128, 1], F32)
                    bhp_ = Buf()
                    hist = sbt(bes, [NS * 3, 128], F32)
                    histT = sbt(bes, [128, NS * 3], F32)
                    h0 = sbt(bes, [NS, 128], F32)
                    h0T = sbt(bes, [128, NS], F32)
                    bhist = Buf()
                    stg16 = sbt(bes, [NS, 2, 128], F32)
                    bstg16 = Buf()

                    def half(hp):
                        wx, wy = preB if hp == 0 else (nextw(1024 + 128 * hp), nextw(1280 + 128 * hp))
                        memset(wr[:], 0.0, [bwr])
                        memset(wi_[:], 0.0, [bwr])
                        for hh in range(2):
                            wload(wr[64 * hh:64 * hh + 64, 64 * hh:64 * hh + 64], w_rgate[l, 2 * hp + hh], bwr)
                            wload(wi_[64 * hh:64 * hh + 64, 64 * hh:64 * hh + 64], w_igate[l, 2 * hp + hh], bwr)
                        memset(UX[:, 0:3], 0.0, [bUX])
                        memset(hprev[:], 0.0, [bhp_])
                        dma_in("sp", hist[:], st_conv[l, :, :, 128 * hp:128 * hp + 128].rearrange("b j c -> (b j) c"),
                               [bhist])
                        dma_in("sp", h0[:], st_lru[l, :, 128 * hp:128 * hp + 128], [bhist])
                        tr(ps[4][:, 0:48], hist[:], ident_f[0:48, 0:48], [bhist, bC], [bPS[4]])
                        tr(ps[4][:, 64:80], h0[:], ident_f[0:16, 0:16], [bhist, bC], [bPS[4]])
                        cp("act", histT[:], ps[4][:, 0:48], [bPS[4]], [bhist])
                        cp("act", h0T[:], ps[4][:, 64:80], [bPS[4]], [bhist])
                        for ti, (c0, c1) in enumerate(TILES):
                            n = c1 - c0
                            T, bT = Tp[ti % 2], bTp[ti % 2]
                            xcb, bxcb = xcbp[ti % 2], bxcbp[ti % 2]
                            proj(wx[0], wx[1], ti, ps[0], bPS[0])
                            proj(wy[0], wy[1], ti, ps[1], bPS[1])
                            cp("act", UX[:, 3:3 + n], ps[0][:, 0:n], [bPS[0]], [bUX])
                            xc = T[0]
                            ts("dve", xc[:, 0:n], UX[:, 3:3 + n], cw[:, l, 3, hp:hp + 1], prm[:, l, P_CB, hp:hp + 1],
                               ALU.mult, ALU.add, [bUX, bC], [bT[0]])
                            for jj in range(3):
                                if ti < 4:
                                    src = UX[:, jj:jj + n]
                                    rd = [bUX]
                                else:
                                    src = histT[:].rearrange("p (b j) -> p b j", j=3)[:, :, jj]
                                    rd = [bhist]
                                stt("dve", xc[:, 0:n], src, cw[:, l, jj, hp:hp + 1], xc[:, 0:n], ALU.mult, ALU.add,
                                    rd + [bT[0], bC], [bT[0]])
                            cp("act", xcb[:, 0:n], xc[:, 0:n], [bT[0]], [bxcb])
                            mm(ps[2][:, 0:n], wr[:], xcb[:, 0:n], True, True, [bwr, bxcb], [bPS[2]])
                            mm(ps[3][:, 0:n], wi_[:], xcb[:, 0:n], True, True, [bwr, bxcb], [bPS[3]])
                            act(T[1][:, 0:n], ps[2][:, 0:n], AF.Sigmoid, [bPS[2], bC], [bT[1]], scale=1.0,
                                bias=prm[:, l, P_BR, hp:hp + 1])
                            act(T[2][:, 0:n], ps[3][:, 0:n], AF.Sigmoid, [bPS[3], bC], [bT[2]], scale=1.0,
                                bias=prm[:, l, P_BI, hp:hp + 1])
                            act(T[3][:, 0:n], ps[1][:, 0:n], AF.Square, [bPS[1]], [bT[3]])
                            ts("dve", T[3][:, 0:n], T[3][:, 0:n], 0.044715, 1.0, ALU.mult, ALU.add, [bT[3]], [bT[3]])
                            tt("dve", T[3][:, 0:n], T[3][:, 0:n], ps[1][:, 0:n], ALU.mult, [bT[3], bPS[1]], [bT[3]])
                            act(T[3][:, 0:n], T[3][:, 0:n], AF.Sigmoid, [bT[3]], [bT[3]], scale=1.5957691216057308)
                            tt("dve", T[3][:, 0:n], T[3][:, 0:n], ps[1][:, 0:n], ALU.mult, [bT[3], bPS[1]], [bT[3]])
                            act(T[4][:, 0:n], T[1][:, 0:n], AF.Exp, [bT[1], bC], [bT[4]], scale=der[:, l, 4, hp:hp + 1])
                            act(T[1][:, 0:n], T[1][:, 0:n], AF.Exp, [bT[1], bC], [bT[1]], scale=der[:, l, 3, hp:hp + 1])
                            ts("dve", T[4][:, 0:n], T[4][:, 0:n], -1.0, 1.0, ALU.mult, ALU.add, [bT[4]], [bT[4]])
                            act(T[4][:, 0:n], T[4][:, 0:n], AF.Ln, [bT[4]], [bT[4]], scale=1.0, bias=1e-30)
                            act(T[4][:, 0:n], T[4][:, 0:n], AF.Exp, [bT[4]], [bT[4]], scale=0.5)
                            tt("dve", T[2][:, 0:n], T[2][:, 0:n], xc[:, 0:n], ALU.mult, [bT[2], bT[0]], [bT[2]])
                            tt("dve", T[2][:, 0:n], T[2][:, 0:n], T[4][:, 0:n], ALU.mult, [bT[2], bT[4]], [bT[2]])
                            H = T[5]
                            if ti < 4:
                                scan(H[:, 0:n], T[1][:, 0:n], T[2][:, 0:n], hprev[:, 0:1], [bT[1], bT[2], bhp_], [bT[5]])
                                cp("dve", hprev[:], H[:, n - 1:n], [bT[5]], [bhp_])
                                cp("dve", UX[:, 0:3], UX[:, n:n + 3], [bUX], [bUX])
                                if ti == 3:
                                    dma("sp", rglru_p[l, 128 * hp:128 * hp + 128].rearrange("(p o) -> p o", o=1),
                                        hprev[:], reads=[bhp_])
                                    dma("sp", conv_p[l, :, 128 * hp:128 * hp + 128].rearrange("j p -> p j"),
                                        UX[:, 0:3], reads=[bUX], nonc=True)
                            else:
                                tt("dve", H[:, 0:n], T[1][:, 0:n], h0T[:], ALU.mult, [bT[1], bhist], [bT[5]])
                                tt("dve", H[:, 0:n], H[:, 0:n], T[2][:, 0:n], ALU.add, [bT[5], bT[2]], [bT[5]])
                                tr(ps[4][0:NS, 0:128], H[:, 0:NS], ident_f[:], [bT[5], bC], [bPS[4]])
                                tr(ps[4][0:NS, 128:256], UX[:, 3:3 + NS], ident_f[:], [bUX, bC], [bPS[4]])
                                cp("act", stg16[:], ps[4][0:NS, 0:256].rearrange("p (a c) -> p a c", c=128),
                                   [bPS[4]], [bstg16])
                                dma("sp", rglru_s[l, :, 128 * hp:128 * hp + 128], stg16[:, 0, :], reads=[bstg16])
                                dma("sp", conv_s[l, :, 2, 128 * hp:128 * hp + 128], stg16[:, 1, :], reads=[bstg16])
                            tt("dve", OT[:, 2 + hp, c0:c1], H[:, 0:n], T[3][:, 0:n], ALU.mult, [bT[5], bT[3]],
                               [bOT[2 + hp][ti]])

                    def tail():
                        dma("sp", conv_s[l, :, 0:2, :], st_conv[l, :, 1:3, :])
                    return half, tail

                def make_D(bes):
                    UX = sbt(bes, [128, 15 + 512], F32)
                    bUX = Buf()
                    SSp = [[sbt(bes, [128, 15 + 512], F32) for _ in range(4)] for _ in range(2)]
                    bSSp = [Buf() for _ in range(2)]
                    PTp = [sbt(bes, [128, 512], F32) for _ in range(2)]
                    bPTp = [Buf() for _ in range(2)]
                    dfbp = [sbt(bes, [128, 512], BF16) for _ in range(2)]
                    bdfbp = [Buf() for _ in range(2)]
                    wp = sbt(bes, [128, 128], BF16)
                    bwp = Buf()
                    hist = sbt(bes, [120, 2, 128], F32)
                    histT = sbt(bes, [128, NS, 15], F32)
                    bhist = Buf()
                    red = sbt(bes, [128, NS], F32)
                    bred = Buf()
                    stg16 = sbt(bes, [NS, 128], F32)
                    bstg16 = Buf()

                    def half(hp):
                        wx = preD[0] if hp == 0 else nextw(2560 + 128 * hp)
                        memset(wp[:], 0.0, [bwp])
                        for hh in range(2):
                            wload(wp[64 * hh:64 * hh + 64, 64 * hh:64 * hh + 64], w_pool[l, 2 * hp + hh], bwp)
                        memset(UX[:, 0:15], 0.0, [bUX])
                        for a in range(2):
                            dma_in("sp", hist[:, a, :],
                                   st_pool[l, 8 * a:8 * a + 8, :, 128 * hp:128 * hp + 128].rearrange(
                                       "b j c -> (b j) c"), [bhist])
                            tr(ps[6][:, 128 * a:128 * a + 120], hist[:, a, :], ident_f[0:120, 0:120], [bhist, bC],
                               [bPS[6]])
                        cp("act", histT[:].rearrange("p (a b) j -> p a (b j)", a=2),
                           ps[6][:, 0:256].rearrange("p (a r) -> p a r", a=2)[:, :, 0:120], [bPS[6]], [bhist])
                        wins = (2 ** (2 * hp + 1), 2 ** (2 * hp + 2))
                        invw = poolc[:, hp, 0:1]
                        for ti, (c0, c1) in enumerate(TILES):
                            n = c1 - c0
                            S2, S4, S8, S16 = SSp[ti % 2]
                            bSS = bSSp[ti % 2]
                            PT, bPT = PTp[ti % 2], bPTp[ti % 2]
                            dfb, bdfb = dfbp[ti % 2], bdfbp[ti % 2]
                            proj(wx[0], wx[1], ti, ps[5], bPS[5])
                            if ti < 4:
                                cp("act", UX[:, 15:15 + n], ps[5][:, 0:n], [bPS[5]], [bUX])
                                m = 15 + n
                                tt("dve", S2[:, 1:m], UX[:, 1:m], UX[:, 0:m - 1], ALU.add, [bUX], [bSS])
                                tt("dve", S4[:, 3:m], S2[:, 3:m], S2[:, 1:m - 2], ALU.add, [bSS], [bSS])
                                if hp == 1:
                                    tt("dve", S8[:, 7:m], S4[:, 7:m], S4[:, 3:m - 4], ALU.add, [bSS], [bSS])
                                    tt("dve", S16[:, 15:m], S8[:, 15:m], S8[:, 7:m - 8], ALU.add, [bSS], [bSS])
                                for hh in range(2):
                                    srcS = {2: S2, 4: S4, 8: S8, 16: S16}[wins[hh]]
                                    ts("dve", PT[64 * hh:64 * hh + 64, 0:n], srcS[64 * hh:64 * hh + 64, 15:15 + n],
                                       poolc[64 * hh:64 * hh + 64, hp, 0:1], None, ALU.mult, None, [bSS, bC], [bPT])
                                if ti == 0:
                                    tt("dve", PT[:, 0:16], PT[:, 0:16], poolc[:, hp, 1:17], ALU.mult, [bPT, bC], [bPT])
                                tt("dve", dfb[:, 0:n], PT[:, 0:n], UX[:, 15:15 + n], ALU.subtract, [bPT, bUX], [bdfb])
                                if ti == 3:
                                    dma("sp", pool_p[l, :, 128 * hp:128 * hp + 128].rearrange("j p -> p j"),
                                        UX[:, n:n + 15], reads=[bUX], nonc=True)
                                cp("pool", UX[:, 0:15], UX[:, n:n + 15], [bUX, bdfb], [bUX])
                            else:
                                cp("act", UX[:, 15:15 + n], ps[5][:, 0:n], [bPS[5]], [bUX])
                                for hh in range(2):
                                    wv_ = wins[hh]
                                    reduce_add(red[64 * hh:64 * hh + 64, :],
                                               histT[64 * hh:64 * hh + 64, :, 15 - (wv_ - 1):15], [bhist], [bred])
                                tt("dve", PT[:, 0:n], red[:], UX[:, 15:15 + n], ALU.add, [bred, bUX], [bPT])
                                ts("dve", PT[:, 0:n], PT[:, 0:n], invw, None, ALU.mult, None, [bPT, bC], [bPT])
                                tt("dve", dfb[:, 0:n], PT[:, 0:n], UX[:, 15:15 + n], ALU.subtract, [bPT, bUX], [bdfb])
                                tr(ps[6][0:NS, 0:128], UX[:, 15:15 + NS], ident_f[:], [bUX, bC], [bPS[6]])
                                cp("act", stg16[:], ps[6][0:NS, 0:128], [bPS[6]], [bstg16])
                                dma("sp", pool_s[l, :, 14, 128 * hp:128 * hp + 128], stg16[:], reads=[bstg16])
                            mm(ps[6][:, 0:n], wp[:], dfb[:, 0:n], True, True, [bwp, bdfb], [bPS[6]])
                            ts("dve", OT[:, 6 + hp, c0:c1], ps[6][:, 0:n], prm[:, l, P_PS, hp:hp + 1], None, ALU.mult,
                               None, [bPS[6], bC], [bOT[6 + hp][ti]])

                    def tail():
                        dma("sp", pool_s[l, :, 0:14, :], st_pool[l, :, 1:15, :])
                    return half, tail

                with ExitStack() as bes:
                    hB, tB = make_B(bes)
                    hD, tD = make_D(bes)
                    def region():
                        for hp in range(2):
                            hB(hp)
                            hD(hp)
                    S.schedule(S.capture(region))
                    tB()
                    tD()
                    preM = [nextw(2816 + b * D) for b in range(4)]
                barrier()
                if stop_after == ("mixD", l):
                    return None

                with ExitStack() as bes:
                    MG = sbt(bes, [128, 4, NT], BF16)
                    acc = sbt(bes, [128, NT], F32)
                    sgt = [sbt(bes, [128, 512], F32) for _ in range(3)]
                    bsgt = [Buf() for _ in range(3)]
                    wbr = [(sbt(bes, [128, 2, 128], BF16), Buf()) for _ in range(4)]
                    wor = [(sbt(bes, [128, 4, 128], BF16), Buf()) for _ in range(4)]
                    bMG = [[Buf() for _ in TILES] for _ in range(4)]
                    bacc = [Buf() for _ in TILES]
                    cnt = 0
                    wcnt = 0
                    ocnt = 0
                    pocnt = 0
                    for dh in range(2):
                        for dcl in range(4):
                            dc = 4 * dh + dcl
                            for b in range(4):
                                wg_ = preM.pop(0) if preM else nextw(2816 + b * D + dc * 128)
                                wb_ = wbr[wcnt % 4]
                                wcnt += 1
                                wload(wb_[0][:], w_branch[l, b][:, dc * 128:(dc + 1) * 128].rearrange(
                                    "(kc p) c -> p kc c", p=128), wb_[1])
                                for ti, (c0, c1) in enumerate(TILES):
                                    n = c1 - c0
                                    pg_, bpg = ps[(0, 1, 4)[cnt % 3]], bPS[(0, 1, 4)[cnt % 3]]
                                    pp_, bpp = ps[(2, 3, 5)[cnt % 3]], bPS[(2, 3, 5)[cnt % 3]]
                                    sg_, bsg = sgt[cnt % 3], bsgt[cnt % 3]
                                    cnt += 1
                                    proj(wg_[0], wg_[1], ti, pg_, bpg)
                                    for kc in range(2):
                                        mm(pp_[:, 0:n], wb_[0][:, kc, :], OT[:, 2 * b + kc, c0:c1], kc == 0, kc == 1,
                                           [wb_[1], bOT[2 * b + kc][ti]], [bpp])
                                    act(sg_[:, 0:n], pg_[:, 0:n], AF.Sigmoid, [bpg], [bsg])
                                    ba = bacc[ti]
                                    if b == 0:
                                        tt("dve", acc[:, c0:c1], sg_[:, 0:n], pp_[:, 0:n], ALU.mult, [bsg, bpp], [ba])
                                    else:
                                        tt("dve", sg_[:, 0:n], sg_[:, 0:n], pp_[:, 0:n], ALU.mult, [bsg, bpp], [bsg])
                                        if b < 3:
                                            tt("dve", acc[:, c0:c1], acc[:, c0:c1], sg_[:, 0:n], ALU.add,
                                               [ba, bsg], [ba])
                                        else:
                                            tt("dve", MG[:, dcl, c0:c1], acc[:, c0:c1], sg_[:, 0:n], ALU.add,
                                               [ba, bsg], [bMG[dcl][ti]])
                        for d2 in range(8):
                            wo_ = wor[ocnt % 4]
                            ocnt += 1
                            wload(wo_[0][:], w_o[l][512 * dh:512 * dh + 512, d2 * 128:(d2 + 1) * 128].rearrange(
                                "(kc p) c -> p kc c", p=128), wo_[1])
                            for ti, (c0, c1) in enumerate(TILES):
                                n = c1 - c0
                                po, bpo = ps[4 + pocnt % 3], bPS[4 + pocnt % 3]
                                pocnt += 1
                                for kc in range(4):
                                    mm(po[:, 0:n], wo_[0][:, kc, :], MG[:, kc, c0:c1], kc == 0, kc == 3,
                                       [wo_[1], bMG[kc][ti]], [bpo])
                                tt("dve", XT[:, d2, c0:c1], XT[:, d2, c0:c1], po[:, 0:n], ALU.add,
                                   [bXT[d2][ti], bpo], [bXT[d2][ti]])
                barrier()
            return None

        dbg = None
        for l in range(L):
            if stop_after is not None and stop_after == ("start", l):
                break
            ffn(ffn1_up[l], ffn1_down[l], 3 * l + 0)
            if stop_after == ("ffn1", l):
                break
            dbg = mixer(l)
            if stop_after is not None and stop_after[1] == l and stop_after[0].startswith("mix"):
                break
            ffn(ffn2_up[l], ffn2_down[l], 3 * l + 2)
            if stop_after == ("ffn2", l):
                break

        with ExitStack() as pes:
            sq = [sbt(pes, [128, 8, 512], BF16) for _ in range(2)]
            bsq = [Buf() for _ in range(2)]
            rs = [sbt(pes, [128, 512], F32) for _ in range(2)]
            brs = [Buf() for _ in range(2)]
            yt = [sbt(pes, [128, 8, 512], F32) for _ in range(2)]
            byt = [Buf() for _ in range(2)]
            ostg = [sbt(pes, [128, D], F32) for _ in range(4)]
            bostg = [Buf() for _ in range(4)]
            def out_region():
                blk_i = 0
                for ti, (c0, c1) in enumerate(TILES):
                    n = c1 - c0
                    s_, bs_ = sq[ti % 2], bsq[ti % 2]
                    r_, br_ = rs[ti % 2], brs[ti % 2]
                    y_, by_ = yt[ti % 2], byt[ti % 2]
                    pb, bpb = ps[4 + ti % 2], bPS[4 + ti % 2]
                    act(s_[:, :, 0:n], XT[:, :, c0:c1], AF.Square, [bXT[k][ti] for k in range(8)], [bs_])
                    for kc in range(8):
                        mm(pb[:, 0:n], ones_b[:], s_[:, kc, 0:n], kc == 0, kc == 7, [bs_, bC], [bpb])
                    act(r_[:, 0:n], pb[:, 0:n], AF.Ln, [bpb], [br_], scale=1.0 / D, bias=EPS)
                    act(r_[:, 0:n], r_[:, 0:n], AF.Exp, [br_], [br_], scale=-0.5)
                    for kc in range(8):
                        stt("dve", y_[:, kc, 0:n], XT[:, kc, c0:c1], gains[:, 6, kc:kc + 1], r_[:, 0:n], ALU.mult,
                            ALU.mult, [bXT[kc][ti], br_, bC], [by_])
                    for q in range((n + 127) // 128):
                        m = min(128, n - q * 128)
                        os_, bos_ = ostg[blk_i % 4], bostg[blk_i % 4]
                        for g in range(2):
                            pt, bpt = ps[(blk_i * 2 + g) % 4], bPS[(blk_i * 2 + g) % 4]
                            for kk in range(4):
                                kc = g * 4 + kk
                                tr(pt[0:m, kk * 128:(kk + 1) * 128], y_[:, kc, q * 128:q * 128 + m], ident_f[:], [by_, bC],
                                   [bpt])
                            cp("act" if g == 0 else "dve", os_[0:m, g * 512:(g + 1) * 512], pt[0:m, :], [bpt], [bos_])
                        r0 = c0 + q * 128
                        dma("sp", y_out[r0:r0 + m, :], os_[0:m, :], reads=[bos_])
                        blk_i += 1

            S.schedule(S.capture(out_region))
        S.finish()
        S.emit(es)
    return nc


_NC_CACHE = {}


def _get_nc():
    if "nc" not in _NC_CACHE:
        _NC_CACHE["nc"] = build_nc()
    return _NC_CACHE["nc"]


def make_in_maps(inputs):
    consts = _const_tables()
    f32 = lambda a: np.ascontiguousarray(np.asarray(a, dtype=np.float32))
    shared = {}
    for k in ["ffn1_up", "ffn1_down", "w_in", "w_rgate", "w_igate", "w_pool", "w_branch", "w_o", "ffn2_up",
              "ffn2_down"]:
        shared[k] = f32(inputs[k])
    fm8 = lambda v: f32(v).reshape(8, 128).T
    fm2 = lambda v: f32(v).reshape(2, 128).T
    gl = [inputs["ffn1_norm"][0], inputs["mix_norm"][0], inputs["ffn2_norm"][0], inputs["ffn1_norm"][1],
          inputs["mix_norm"][1], inputs["ffn2_norm"][1], inputs["final_norm"]]
    shared["gains_fm"] = np.ascontiguousarray(np.stack([fm8(v) for v in gl], axis=1))
    prm = np.zeros((128, L, 12, 2), np.float32)
    cwf = np.zeros((128, L, 4, 2), np.float32)
    for l in range(L):
        plist = [inputs["lb_logits"][0], inputs["lb_logits"][1], inputs["hgrn_norm"][l], inputs["conv_b"][l],
                 inputs["b_rgate"][l], inputs["b_igate"][l], inputs["lru_lambda"][l], inputs["ret_norm"][l],
                 inputs["pool_scale"][l]]
        for idx, v in enumerate(plist):
            prm[:, l, idx, :] = fm2(v)
        for jj in range(4):
            cwf[:, l, jj, :] = fm2(inputs["conv_w"][l, jj])
    shared["prm_fm"] = prm
    shared["cw_fm"] = cwf
    for k, v in consts.items():
        shared[k] = np.ascontiguousarray(v)
    xp = f32(inputs["x_prompt"])
    xs = f32(inputs["x_sample"])
    in_maps = []
    for i in range(NCORES):
        m = dict(shared)
        sl = slice(NS * i, NS * (i + 1))
        m["xin"] = np.ascontiguousarray(np.concatenate([xp[i], xs[sl, 0, :]], axis=0))
        m["st_hgrn"] = f32(inputs["state_hgrn"][:, sl])
        m["st_lru"] = f32(inputs["state_rglru"][:, sl])
        m["st_conv"] = f32(inputs["state_conv"][:, sl])
        m["st_ret"] = f32(inputs["state_retention"][:, sl])
        m["st_pool"] = f32(inputs["state_pool"][:, sl])
        in_maps.append(m)
    return in_maps


def assemble(results):
    y = [r["y"] for r in results]
    y_prompt = np.stack([a[:NP_] for a in y], axis=0)
    y_sample = np.concatenate([a[NP_:] for a in y], axis=0)[:, None, :]
    outs = [y_prompt, y_sample]
    for k in ["hgrn_p", "rglru_p", "conv_p", "ret_p", "pool_p"]:
        outs.append(np.stack([r[k] for r in results], axis=1))
    for k in ["hgrn_s", "rglru_s", "conv_s", "ret_s", "pool_s"]:
        outs.append(np.concatenate([r[k] for r in results], axis=1))
    return tuple(np.ascontiguousarray(o.astype(np.float32)) for o in outs)


def kernel(**inputs):
    nc = _get_nc()
    in_maps = make_in_maps(inputs)
    res = run_bass_kernel_spmd(nc, in_maps, core_ids=list(range(NCORES)))
    return assemble(res.results)
```

```python
from contextlib import ExitStack
import numpy as np
import ml_dtypes
import concourse.bass as bass
import concourse.mybir as mybir
from concourse.bass_utils import run_bass_kernel_spmd

F32 = mybir.dt.float32
BF16 = mybir.dt.bfloat16
AF = mybir.ActivationFunctionType
ALU = mybir.AluOpType
AX = mybir.AxisListType

ENGS = ("pe", "act", "dve", "pool", "sp")
SEM_LIMIT = 30000
DMA_POOL = 12

D = 1024
NP_ = 2048
NS = 16
NT = NP_ + NS
DFF = 2816
BW = 256
L = 2
EPS = 1e-6
TILES = [(0, 512), (512, 1024), (1024, 1536), (1536, 2048), (2048, 2064)]
NCORES = 8


class Buf:
    __slots__ = ("name", "w", "r")

    def __init__(self, name=""):
        self.name = name
        self.w = None
        self.r = {}


class Op:
    __slots__ = ("eng", "fn", "dma", "idx", "deps", "signal", "sem", "val", "prev_same_sem")

    def __init__(self, eng, fn, dma, idx):
        self.eng = eng
        self.fn = fn
        self.dma = dma
        self.idx = idx
        self.deps = []
        self.signal = False
        self.sem = None
        self.val = 0
        self.prev_same_sem = None


class Sched:
    def __init__(self, nc):
        self.nc = nc
        self.ops = {e: [] for e in ENGS}
        self.dma_ops = {e: [] for e in ENGS}
        self.dma_since_barrier = {e: [] for e in ENGS}
        self._cap = None
        self.sim_free = {e: 0.0 for e in ENGS}
        self.sim_tag = None
        self.sim_w = {}
        self.sim_r = {}

    def capture(self, f):
        assert self._cap is None
        self._cap = []
        f()
        lst = self._cap
        self._cap = None
        return lst

    def _sim_start(self, eng, reads, writes):
        t = self.sim_free[eng]
        for b in reads:
            t = max(t, self.sim_w.get(id(b), 0.0))
        for b in writes:
            t = max(t, self.sim_w.get(id(b), 0.0), self.sim_r.get(id(b), 0.0))
        return t

    def _sim_commit(self, eng, reads, writes, cost):
        st = self._sim_start(eng, reads, writes)
        en = st + cost
        self.sim_free[eng] = st + (0.05 if eng in ("sp",) else cost)
        for b in reads:
            self.sim_r[id(b)] = max(self.sim_r.get(id(b), 0.0), en)
        for b in writes:
            self.sim_w[id(b)] = en + 0.15
            self.sim_r[id(b)] = 0.0

    def schedule(self, lst):
        n = len(lst)
        lastw, readers = {}, {}
        preds = [set() for _ in range(n)]
        lastdma = {}
        for i, a in enumerate(lst):
            eng, reads, writes, dma = a[0], a[2], a[3], a[4]
            for b in reads:
                if id(b) in lastw:
                    preds[i].add(lastw[id(b)])
            for b in writes:
                if id(b) in lastw:
                    preds[i].add(lastw[id(b)])
                for r in readers.get(id(b), ()):
                    preds[i].add(r)
            if dma:
                if eng in lastdma:
                    preds[i].add(lastdma[eng])
                lastdma[eng] = i
            for b in reads:
                readers.setdefault(id(b), []).append(i)
            for b in writes:
                lastw[id(b)] = i
                readers[id(b)] = []
            preds[i].discard(i)
        succs = [[] for _ in range(n)]
        indeg = [0] * n
        for i in range(n):
            indeg[i] = len(preds[i])
            for p in preds[i]:
                succs[p].append(i)
        blev = [0.0] * n
        for i in range(n - 1, -1, -1):
            m = 0.0
            for j in succs[i]:
                if blev[j] > m:
                    m = blev[j]
            blev[i] = m + lst[i][6] + 0.15
        ready = [i for i in range(n) if indeg[i] == 0]
        done = 0
        while ready:
            best, bk = None, None
            for i in ready:
                a = lst[i]
                st = self._sim_start(a[0], a[2], a[3])
                if a[0] == "act" and a[7] is not None and a[7] != self.sim_tag:
                    st += 1.3
                key = (int(st / 0.4), -blev[i], i)
                if bk is None or key < bk:
                    best, bk = i, key
            ready.remove(best)
            self.op(*lst[best])
            done += 1
            for j in succs[best]:
                indeg[j] -= 1
                if indeg[j] == 0:
                    ready.append(j)
        assert done == n

    def replay(self, *lists):
        pos = [0] * len(lists)
        tot = sum(len(x) for x in lists)
        for _ in range(tot):
            best, bk = None, None
            for i, x in enumerate(lists):
                if pos[i] < len(x):
                    a = x[pos[i]]
                    key = (self._sim_start(a[0], a[2], a[3]), pos[i] / len(x))
                    if bk is None or key < bk:
                        best, bk = i, key
            a = lists[best][pos[best]]
            pos[best] += 1
            self.op(*a)

    def op(self, eng, fn, reads=(), writes=(), dma=False, extra=(), cost=None, tag=None, nobar=False):
        if cost is None:
            cost = 2.0 if dma else {"pe": 0.25, "act": 0.6, "dve": 0.6, "pool": 1.0, "sp": 0.05}[eng]
        if self._cap is not None:
            self._cap.append((eng, fn, tuple(reads), tuple(writes), dma, tuple(extra), cost, tag, nobar))
            return None
        if tag is not None and eng == "act":
            if self.sim_tag != tag:
                cost = cost + 1.3
                self.sim_tag = tag
        self._sim_commit(eng, reads, writes, cost)
        o = Op(eng, fn, dma, len(self.ops[eng]))
        deps = {}
        for b in reads:
            if b.w is not None:
                deps[id(b.w)] = b.w
        for b in writes:
            if b.w is not None:
                deps[id(b.w)] = b.w
            for r in b.r.values():
                deps[id(r)] = r
        for d in extra:
            deps[id(d)] = d
        for d in deps.values():
            if d is o:
                continue
            if d.eng == "pe" and eng == "pe" and not d.dma and not dma:
                continue
            d.signal = True
            o.deps.append(d)
        for b in reads:
            key = (eng, o.idx) if dma else eng
            b.r[key] = o
        for b in writes:
            b.w = o
            b.r = {}
        if dma:
            o.signal = True
            lst = self.dma_ops[eng]
            n = len(lst)
            if n >= DMA_POOL:
                o.prev_same_sem = lst[n - DMA_POOL]
            lst.append(o)
            if not nobar:
                self.dma_since_barrier[eng].append(o)
        self.ops[eng].append(o)
        return o

    def barrier(self, scratch):
        mk = {
            "act": lambda e: e.activation(out=scratch["act"], in_=scratch["src"], func=AF.Copy),
            "dve": lambda e: e.tensor_copy(out=scratch["dve"], in_=scratch["src"]),
            "pool": lambda e: e.tensor_copy(out=scratch["pool"], in_=scratch["src"]),
            "sp": lambda e: e.nop(),
            "pe": lambda en: en.matmul(scratch["ps"], lhsT=scratch["mm"], rhs=scratch["mm"], start=True, stop=True),
        }
        firsts = []
        for e in ENGS:
            ex = list(self.dma_since_barrier[e])
            self.dma_since_barrier[e] = []
            wr = list(scratch["w"][e]) + (list(scratch["pe_writes"]) if e == "pe" else [])
            o = self.op(e, mk[e], extra=ex, reads=scratch["reads"], writes=wr)
            o.signal = True
            firsts.append(o)
        for e in ENGS:
            wr = list(scratch["w"][e]) + (list(scratch["pe_writes"]) if e == "pe" else [])
            self.op(e, mk[e], extra=[f for f in firsts if f.eng != e], reads=scratch["reads"], writes=wr)

    def finish(self):
        o = Op("sp", lambda e: e.nop(), False, len(self.ops["sp"]))
        for e in ENGS:
            for d in self.dma_ops[e]:
                o.deps.append(d)
        self.ops["sp"].append(o)

    def emit(self, es):
        nc = self.nc
        nsem = {}
        for e in ENGS:
            cnt = sum(1 for o in self.ops[e] if o.signal and not o.dma)
            nsem[e] = max(1, (cnt + SEM_LIMIT - 1) // SEM_LIMIT)
        sems = {e: [es.enter_context(nc.semaphore(f"s_{e}_{i}")) for i in range(nsem[e])] for e in ENGS}
        dsems = {}
        for e in ENGS:
            if self.dma_ops[e]:
                dsems[e] = [es.enter_context(nc.semaphore(f"d_{e}_{i}")) for i in range(DMA_POOL)]
        for e in ENGS:
            c = 0
            for o in self.ops[e]:
                if o.dma:
                    continue
                if o.signal:
                    o.sem = sems[e][c // SEM_LIMIT]
                    o.val = c % SEM_LIMIT + 1
                    c += 1
            for n, o in enumerate(self.dma_ops[e]):
                o.sem = dsems[e][n % DMA_POOL]
                o.val = 16 * (n // DMA_POOL + 1)
        block = es.enter_context(nc.Block())
        starter = {"pe": block.tensor, "act": block.scalar, "dve": block.vector, "pool": block.gpsimd,
                   "sp": block.sync}
        for e in ENGS:
            ops = self.ops[e]
            if not ops:
                continue

            def body(eng_obj, ops=ops):
                known = {}
                for o in ops:
                    waits = {}
                    dl = o.deps
                    if o.prev_same_sem is not None:
                        dl = dl + [o.prev_same_sem]
                    for d in dl:
                        k = id(d.sem)
                        if known.get(k, 0) >= d.val:
                            continue
                        if k not in waits or waits[k][1] < d.val:
                            waits[k] = (d.sem, d.val)
                    for k, (sem, val) in waits.items():
                        eng_obj.wait_ge(sem, val)
                        known[k] = val
                    inst = o.fn(eng_obj)
                    if o.signal:
                        inst.then_inc(o.sem, 16 if o.dma else 1)

            starter[e](body)


def _const_tables():
    c = {}
    c["ident_f"] = np.eye(128, dtype=np.float32)
    c["ident_b"] = np.eye(128, dtype=np.float32).astype(ml_dtypes.bfloat16)
    c["ones_b"] = np.ones((128, 128), np.float32).astype(ml_dtypes.bfloat16)
    bo = np.zeros((128, 128), np.float32)
    bo[:64, :64] = 1
    bo[64:, 64:] = 1
    c["bones_b"] = bo.astype(ml_dtypes.bfloat16)
    s = np.arange(128)
    t = np.arange(128)
    c["cmask"] = (t[None, :] >= s[:, None]).astype(np.float32)
    hm = np.zeros((128, 2), np.float32)
    hm[:64, 0] = 1
    hm[64:, 1] = 1
    c["hmask"] = hm
    pm = np.zeros((128, 128), np.float32)
    for p in range(128):
        pm[p, p ^ 32] = 1.0
    c["permf"] = pm
    rm = np.ones((128, 512), np.float32)
    rm[:, ::64] = 0
    c["rmask"] = rm.astype(ml_dtypes.bfloat16)
    c["zmask"] = np.zeros((128, 16), np.float32)
    pos = np.concatenate([np.arange(NP_, dtype=np.float32), np.full(NS, 16384.0, np.float32)])
    j = np.concatenate([np.arange(NP_) % 64, np.zeros(NS)]).astype(np.float64)
    half = 32
    freq = (10000.0 ** (-np.arange(half, dtype=np.float32) / half)).astype(np.float32)
    tabs = np.zeros((2, 4, 128, NT), np.float32)
    gl = np.zeros((2, 128, 3), np.float32)
    for hp in range(2):
        for p in range(128):
            h = 2 * hp + p // 64
            d = p % 64
            ang = (pos * freq[d % 32]).astype(np.float32)
            cs = np.cos(ang.astype(np.float64))
            sn = np.sin(ang.astype(np.float64))
            sgn = -1.0 if d < 32 else 1.0
            lg = np.log1p(-(2.0 ** (-5.0 - h)))
            d1 = np.exp(lg * (j + 1))
            d3 = np.exp(-lg * (j + 1)) / 8.0
            tabs[hp, 0, p] = cs * d1
            tabs[hp, 1, p] = sgn * sn * d1
            tabs[hp, 2, p] = cs * d3
            tabs[hp, 3, p] = sgn * sn * d3
            gl[hp, p, 0] = np.exp(lg * 64)
            gl[hp, p, 1] = np.exp(lg)
            gl[hp, p, 2] = np.exp(-lg)
    c["rtab"] = tabs
    c["gam"] = gl
    pw = np.zeros((2, 128, 17), np.float32)
    for hp in range(2):
        for p in range(128):
            win = 2 ** (2 * hp + p // 64 + 1)
            pw[hp, p, 0] = 1.0 / win
            tt = np.arange(16)
            pw[hp, p, 1:] = win / np.minimum(win, tt + 1.0)
    c["poolc"] = pw
    return c


def build_nc(stop_after=None):
    nc = bass.Bass("TRN2", target_bir_lowering=False)
    dram_in = {}

    def din(name, shape, dt=F32):
        dram_in[name] = nc.dram_tensor(name, list(shape), dt, kind="ExternalInput").ap()
        return dram_in[name]

    def dout(name, shape):
        return nc.dram_tensor(name, list(shape), F32, kind="ExternalOutput").ap()

    xin = din("xin", [NT, D])
    st_hgrn = din("st_hgrn", [L, NS, 4, 64, 64])
    st_lru = din("st_lru", [L, NS, BW])
    st_conv = din("st_conv", [L, NS, 3, BW])
    st_ret = din("st_ret", [L, NS, 4, 64, 64])
    st_pool = din("st_pool", [L, NS, 15, BW])
    ffn1_up = din("ffn1_up", [L, D, 2 * DFF])
    ffn1_down = din("ffn1_down", [L, DFF, D])
    w_in = din("w_in", [L, D, 6912])
    w_rgate = din("w_rgate", [L, 4, 64, 64])
    w_igate = din("w_igate", [L, 4, 64, 64])
    w_pool = din("w_pool", [L, 4, 64, 64])
    w_branch = din("w_branch", [L, 4, BW, D])
    w_o = din("w_o", [L, D, D])
    ffn2_up = din("ffn2_up", [L, D, 2 * DFF])
    ffn2_down = din("ffn2_down", [L, DFF, D])
    c_gains_fm = din("gains_fm", [128, 7, 8])
    c_prm_fm = din("prm_fm", [128, L, 12, 2])
    c_cw_fm = din("cw_fm", [128, L, 4, 2])
    c_ident_f = din("ident_f", [128, 128])
    c_ident_b = din("ident_b", [128, 128], BF16)
    c_ones_b = din("ones_b", [128, 128], BF16)
    c_bones_b = din("bones_b", [128, 128], BF16)
    c_cmask = din("cmask", [128, 128])
    c_hmask = din("hmask", [128, 2])
    c_permf = din("permf", [128, 128])
    c_rmask = din("rmask", [128, 512], BF16)
    c_zmask = din("zmask", [128, 16])
    c_rtab = din("rtab", [2, 4, 128, NT])
    c_gam = din("gam", [2, 128, 3])
    c_poolc = din("poolc", [2, 128, 17])

    y_out = dout("y", [NT, D])
    hgrn_p = dout("hgrn_p", [L, 4, 64, 64])
    rglru_p = dout("rglru_p", [L, BW])
    conv_p = dout("conv_p", [L, 3, BW])
    ret_p = dout("ret_p", [L, 4, 64, 64])
    pool_p = dout("pool_p", [L, 15, BW])
    hgrn_s = dout("hgrn_s", [L, NS, 4, 64, 64])
    rglru_s = dout("rglru_s", [L, NS, BW])
    conv_s = dout("conv_s", [L, NS, 3, BW])
    ret_s = dout("ret_s", [L, NS, 4, 64, 64])
    pool_s = dout("pool_s", [L, NS, 15, BW])

    es = ExitStack()
    with es:
        S = Sched(nc)
        nbuf = [0]

        def sbt(es_, shape, dt, name=None):
            nbuf[0] += 1
            return es_.enter_context(nc.sbuf_tensor(f"sb{nbuf[0]}_{name or 't'}", list(shape), dt))

        XT = sbt(es, [128, 8, NT], F32, "XT")
        XN = sbt(es, [128, 8, NT], BF16, "XN")
        bXT = [[Buf() for _ in TILES] for _ in range(8)]
        bXN = [[Buf() for _ in TILES] for _ in range(8)]
        ident_f = sbt(es, [128, 128], F32, "ident_f")
        ident_b = sbt(es, [128, 128], BF16, "ident_b")
        ones_b = sbt(es, [128, 128], BF16, "ones_b")
        bones_b = sbt(es, [128, 128], BF16, "bones_b")
        cmask = sbt(es, [128, 128], F32, "cmask")
        hmask = sbt(es, [128, 2], F32, "hmask")
        permf = sbt(es, [128, 128], F32, "permf")
        rmask = sbt(es, [128, 512], BF16, "rmask")
        zmask = sbt(es, [128, 16], F32, "zmask")
        gam = sbt(es, [128, 2, 3], F32, "gam")
        poolc = sbt(es, [128, 2, 17], F32, "poolc")
        gains = sbt(es, [128, 7, 8], F32, "gains")
        prm = sbt(es, [128, L, 12, 2], F32, "prm")
        cw = sbt(es, [128, L, 4, 2], F32, "cw")
        der = sbt(es, [128, L, 8, 2], F32, "der")
        scr = sbt(es, [128, 8], F32, "scr")
        scr_b = sbt(es, [128, 8], BF16, "scr_b")
        bC = Buf("consts")
        bScr = Buf("scr")
        P_LB0, P_LB1, P_HG, P_CB, P_BR, P_BI, P_LAM, P_RN, P_PS = range(9)

        ps = [es.enter_context(nc.psum_tensor(f"ps{i}", [128, 512], F32)) for i in range(7)]
        psb = es.enter_context(nc.psum_tensor("psb", [128, 1024], BF16))
        bPS = [Buf(f"ps{i}") for i in range(7)]
        bPSB = Buf("psb")

        def dma_in(eng, dst, src, bufs, nonc=False):
            if nonc:
                return S.op(eng, lambda e: e.dma_start(out=dst, in_=src, allow_slow_non_contiguous=True),
                            writes=bufs, dma=True)
            return S.op(eng, lambda e: e.dma_start(out=dst, in_=src), writes=bufs, dma=True)

        dma_in("sp", ident_f[:], c_ident_f, [bC])
        dma_in("sp", ident_b[:], c_ident_b, [bC])
        dma_in("sp", ones_b[:], c_ones_b, [bC])
        dma_in("sp", bones_b[:], c_bones_b, [bC])
        dma_in("sp", cmask[:], c_cmask, [bC])
        dma_in("sp", hmask[:], c_hmask, [bC])
        dma_in("sp", permf[:], c_permf, [bC])
        dma_in("sp", rmask[:], c_rmask, [bC])
        dma_in("sp", zmask[:], c_zmask, [bC])
        dma_in("sp", gam[:], c_gam.rearrange("h p t -> p h t"), [bC])
        dma_in("sp", poolc[:], c_poolc.rearrange("h p t -> p h t"), [bC])
        dma_in("sp", gains[:], c_gains_fm, [bC])
        dma_in("sp", prm[:], c_prm_fm, [bC])
        dma_in("sp", cw[:], c_cw_fm, [bC])
        S.op("dve", lambda e: e.memset(scr[:], 0.0), writes=[bScr])
        S.op("dve", lambda e: e.memset(scr_b[:], 0.0), writes=[bScr])
        bar_scr = {"src": scr[:, 0:1], "act": scr[:, 1:2], "dve": scr[:, 2:3], "pool": scr[:, 3:4],
                   "mm": scr_b[:, 0:8], "ps": ps[6][0:8, 0:8], "reads": [bScr], "pe_writes": [bPS[6]],
                   "w": {"pe": [], "sp": [], "act": [Buf("bar_act")], "dve": [Buf("bar_dve")],
                         "pool": [Buf("bar_pool")]}}

        def barrier():
            S.barrier(bar_scr)

        for l in range(L):
            if l == 0:
                S.op("dve", lambda e: e.memset(der[:, 0, 0, :], 0.0), reads=[bC], writes=[bC])
            else:
                S.op("dve", lambda e: e.tensor_tensor(out=der[:, 1, 0, :], in0=prm[:, 1, P_LB1, :],
                                                      in1=prm[:, 1, P_LB0, :], op=ALU.subtract),
                     reads=[bC], writes=[bC])
                S.op("act", lambda e: e.activation(out=der[:, 1, 0, :], in_=der[:, 1, 0, :], func=AF.Sigmoid),
                     reads=[bC], writes=[bC])
            S.op("dve", lambda e, l=l: e.tensor_scalar(out=der[:, l, 1, :], in0=der[:, l, 0, :], scalar1=-1.0,
                                                       scalar2=1.0, op0=ALU.mult, op1=ALU.add),
                 reads=[bC], writes=[bC])
            S.op("dve", lambda e, l=l: e.tensor_scalar(out=der[:, l, 2, :], in0=der[:, l, 0, :], scalar1=1.0,
                                                       scalar2=-1.0, op0=ALU.mult, op1=ALU.add),
                 reads=[bC], writes=[bC])
            S.op("act", lambda e, l=l: e.activation(out=der[:, l, 3, :], in_=prm[:, l, P_LAM, :], func=AF.Exp,
                                                    scale=-1.0), reads=[bC], writes=[bC])
            S.op("act", lambda e, l=l: e.activation(out=der[:, l, 3, :], in_=der[:, l, 3, :], func=AF.Ln,
                                                    scale=1.0, bias=1.0), reads=[bC], writes=[bC])
            S.op("dve", lambda e, l=l: e.tensor_scalar(out=der[:, l, 3, :], in0=der[:, l, 3, :], scalar1=-8.0,
                                                       scalar2=None, op0=ALU.mult), reads=[bC], writes=[bC])
            S.op("dve", lambda e, l=l: e.tensor_scalar(out=der[:, l, 4, :], in0=der[:, l, 3, :], scalar1=2.0,
                                                       scalar2=None, op0=ALU.mult), reads=[bC], writes=[bC])

        def fsz(ap):
            n = 1
            for d in ap.shape[1:]:
                n *= d
            return n

        def mm(out, lhsT, rhs, start, stop, reads, writes, tp=None):
            c = 0.1 + fsz(rhs) / 1700.0
            if tp is None:
                S.op("pe", lambda e: e.matmul(out, lhsT=lhsT, rhs=rhs, start=start, stop=stop),
                     reads=reads, writes=writes, cost=c)
            else:
                S.op("pe", lambda e: e.matmul(out, lhsT=lhsT, rhs=rhs, start=start, stop=stop, tile_position=tp),
                     reads=reads, writes=writes, cost=c)

        def act(out, in_, func, reads, writes, scale=None, bias=None):
            kw = {}
            if scale is not None:
                kw["scale"] = scale
            if bias is not None:
                kw["bias"] = bias
            tag = {AF.Sigmoid: "sig", AF.Silu: "silu", AF.Ln: "lnexp", AF.Exp: "lnexp"}.get(func)
            S.op("act", lambda e: e.activation(out=out, in_=in_, func=func, **kw), reads=reads, writes=writes,
                 cost=0.25 + fsz(out) / 1100.0, tag=tag)

        def sigm(out, in_, reads, writes):
            act(out, in_, AF.Exp, reads, writes, scale=-1.0)
            act(out, out, AF.Ln, writes, writes, scale=1.0, bias=1.0)
            act(out, out, AF.Exp, writes, writes, scale=-1.0)

        def ecost(eng, out):
            if eng == "pool":
                return 0.3 + fsz(out) / 450.0
            if eng == "act":
                return 0.25 + fsz(out) / 1100.0
            return 0.15 + fsz(out) / 900.0

        def tt(eng, out, in0, in1, op, reads, writes):
            S.op(eng, lambda e: e.tensor_tensor(out=out, in0=in0, in1=in1, op=op), reads=reads, writes=writes,
                 cost=ecost(eng, out))

        def ts(eng, out, in0, s1, s2, op0, op1, reads, writes):
            if op1 is None:
                S.op(eng, lambda e: e.tensor_scalar(out=out, in0=in0, scalar1=s1, scalar2=None, op0=op0),
                     reads=reads, writes=writes, cost=ecost(eng, out))
            else:
                S.op(eng, lambda e: e.tensor_scalar(out=out, in0=in0, scalar1=s1, scalar2=s2, op0=op0, op1=op1),
                     reads=reads, writes=writes, cost=ecost(eng, out))

        def stt(eng, out, in0, scalar, in1, op0, op1, reads, writes):
            S.op(eng, lambda e: e.scalar_tensor_tensor(out=out, in0=in0, scalar=scalar, in1=in1, op0=op0, op1=op1),
                 reads=reads, writes=writes, cost=ecost(eng, out))

        def cp(eng, out, in_, reads, writes):
            if eng == "act":
                S.op("act", lambda e: e.activation(out=out, in_=in_, func=AF.Copy), reads=reads, writes=writes,
                     cost=ecost("act", out))
            else:
                S.op(eng, lambda e: e.tensor_copy(out=out, in_=in_), reads=reads, writes=writes,
                     cost=ecost(eng, out))

        def dma(eng, out, in_, reads=(), writes=(), nonc=False):
            if nonc:
                return S.op(eng, lambda e: e.dma_start(out=out, in_=in_, allow_slow_non_contiguous=True),
                            reads=reads, writes=writes, dma=True)
            return S.op(eng, lambda e: e.dma_start(out=out, in_=in_), reads=reads, writes=writes, dma=True)

        def tr(out, in_, ident, reads, writes):
            S.op("pe", lambda e: e.transpose(out=out, in_=in_, identity=ident), reads=reads, writes=writes, cost=0.15)

        def scan(out, d0, d1, init, reads, writes):
            S.op("dve", lambda e: e.tensor_tensor_scan(out=out, data0=d0, data1=d1, initial=init, op0=ALU.mult,
                                                       op1=ALU.add), reads=reads, writes=writes,
                 cost=0.2 + fsz(out) / 450.0)

        def memset(ap, val, writes):
            S.op("dve", lambda e: e.memset(ap, val), writes=writes)

        def reduce_add(out, in_, reads, writes):
            S.op("dve", lambda e: e.tensor_reduce(out=out, in_=in_, axis=AX.X, op=ALU.add), reads=reads,
                 writes=writes)

        def wload(dst, src, buf, nobar=False):
            return S.op("pool", lambda e: e.dma_start(out=dst, in_=src), writes=[buf], dma=True, nobar=nobar)

        with ExitStack() as pes:
            NSTG = 17
            stg = [sbt(pes, [128, D], F32) for _ in range(NSTG)]
            bstg = [Buf() for _ in range(NSTG)]
            nblk = 17

            def in_region():
              for blk in range(nblk):
                r0 = blk * 128
                n = 128 if blk < 16 else NS
                ti = min(blk // 4, 4)
                st_, bs_ = stg[blk % NSTG], bstg[blk % NSTG]
                dma_in("sp", st_[0:n, :], xin[r0:r0 + n, :], [bs_])
                for g in range(2):
                    pb, bpb = ps[(blk * 2 + g) % 6], bPS[(blk * 2 + g) % 6]
                    for q in range(4):
                        kc = g * 4 + q
                        tr(pb[:, q * 128:q * 128 + n], st_[0:n, kc * 128:(kc + 1) * 128], ident_f[0:n, 0:n], [bs_, bC],
                           [bpb])
                    src = pb[:].rearrange("p (q t) -> p q t", t=128)[:, :, 0:n]
                    cp("act" if g == 0 else "dve", XT[:, g * 4:g * 4 + 4, r0:r0 + n], src, [bpb],
                       [bXT[kc][ti] for kc in range(g * 4, g * 4 + 4)])

            S.schedule(S.capture(in_region))
        barrier()

        def rmsnorm(pes, gi):
            sq = [sbt(pes, [128, 8, 512], BF16) for _ in range(2)]
            bsq = [Buf() for _ in range(2)]
            rs = [sbt(pes, [128, 512], F32) for _ in range(2)]
            brs = [Buf() for _ in range(2)]

            def square(ti):
                c0, c1 = TILES[ti]
                act(sq[ti % 2][:, :, 0:c1 - c0], XT[:, :, c0:c1], AF.Square, [bXT[k][ti] for k in range(8)],
                    [bsq[ti % 2]])

            square(0)
            for ti, (c0, c1) in enumerate(TILES):
                n = c1 - c0
                s_, bs_ = sq[ti % 2], bsq[ti % 2]
                r_, br_ = rs[ti % 2], brs[ti % 2]
                pb, bpb = ps[4 + ti % 2], bPS[4 + ti % 2]
                for kc in range(8):
                    mm(pb[:, 0:n], ones_b[:], s_[:, kc, 0:n], kc == 0, kc == 7, [bs_, bC], [bpb])
                if ti + 1 < len(TILES):
                    square(ti + 1)
                act(r_[:, 0:n], pb[:, 0:n], AF.Ln, [bpb], [br_], scale=1.0 / D, bias=EPS)
                act(r_[:, 0:n], r_[:, 0:n], AF.Exp, [br_], [br_], scale=-0.5)
                for kc in range(8):
                    stt("dve", XN[:, kc, c0:c1], XT[:, kc, c0:c1], gains[:, gi, kc:kc + 1], r_[:, 0:n],
                        ALU.mult, ALU.mult, [bXT[kc][ti], br_, bC], [bXN[kc][ti]])

        def ffn(w_up, w_down, gi):
            with ExitStack() as pes:
                rmsnorm(pes, gi)
                NR = 2
                wa = [sbt(pes, [128, 8, 512], BF16) for _ in range(NR)]
                wb = [sbt(pes, [128, 8, 512], BF16) for _ in range(NR)]
                wd = [sbt(pes, [128, 4, D], BF16) for _ in range(NR)]
                bwa = [Buf() for _ in range(NR)]
                bwb = [Buf() for _ in range(NR)]
                bwd = [Buf() for _ in range(NR)]
                hg = sbt(pes, [128, 8, NT], BF16)
                bhg = [[Buf() for _ in TILES] for _ in range(8)]
                sa = [sbt(pes, [128, 512], F32) for _ in range(3)]
                bsa = [Buf() for _ in range(3)]
                cnt = 0
                ocnt = 0
                for g in range(6):
                    wdt = 512 if g < 5 else 256
                    nfc = wdt // 128
                    r = g % NR
                    hoff = 4 * (g % 2)
                    wload(wa[r][:, :, 0:wdt], w_up[:, 512 * g:512 * g + wdt].rearrange("(kc p) c -> p kc c", p=128),
                          bwa[r])
                    wload(wb[r][:, :, 0:wdt],
                          w_up[:, DFF + 512 * g:DFF + 512 * g + wdt].rearrange("(kc p) c -> p kc c", p=128), bwb[r])
                    wload(wd[r][:, 0:nfc, :], w_down[512 * g:512 * g + wdt, :].rearrange("(fc p) c -> p fc c", p=128),
                          bwd[r])
                    for ti, (c0, c1) in enumerate(TILES):
                        n = c1 - c0
                        for fc in range(nfc):
                            pa, bpa = ps[(0, 1, 4)[cnt % 3]], bPS[(0, 1, 4)[cnt % 3]]
                            pb, bpb = ps[(2, 3, 5)[cnt % 3]], bPS[(2, 3, 5)[cnt % 3]]
                            s_, bs_ = sa[cnt % 3], bsa[cnt % 3]
                            cnt += 1
                            for kc in range(8):
                                mm(pa[:, 0:n], wa[r][:, kc, fc * 128:(fc + 1) * 128], XN[:, kc, c0:c1], kc == 0,
                                   kc == 7, [bwa[r], bXN[kc][ti]], [bpa])
                            for kc in range(8):
                                mm(pb[:, 0:n], wb[r][:, kc, fc * 128:(fc + 1) * 128], XN[:, kc, c0:c1], kc == 0,
                                   kc == 7, [bwb[r], bXN[kc][ti]], [bpb])
                            act(s_[:, 0:n], pa[:, 0:n], AF.Silu, [bpa], [bs_])
                            tt("dve", hg[:, hoff + fc, c0:c1], s_[:, 0:n], pb[:, 0:n], ALU.mult, [bs_, bpb],
                               [bhg[hoff + fc][ti]])
                    if g % 2 == 0:
                        continue
                    dsrc = [((g - 1) % NR, fc, fc) for fc in range(4)] + [(r, fc, 4 + fc) for fc in range(nfc)]
                    for ti, (c0, c1) in enumerate(TILES):
                        n = c1 - c0
                        for dc in range(8):
                            po, bpo = ps[4 + ocnt % 3], bPS[4 + ocnt % 3]
                            ocnt += 1
                            for i_, (rr, fw, fh) in enumerate(dsrc):
                                mm(po[:, 0:n], wd[rr][:, fw, dc * 128:(dc + 1) * 128], hg[:, fh, c0:c1], i_ == 0,
                                   i_ == len(dsrc) - 1, [bwd[rr], bhg[fh][ti]], [bpo])
                            stt("dve", XT[:, dc, c0:c1], po[:, 0:n], 0.5, XT[:, dc, c0:c1], ALU.mult, ALU.add,
                                [bpo, bXT[dc][ti]], [bXT[dc][ti]])
            barrier()

        def load_wchunk(pes_ring, l, col0):
            t_, b_ = pes_ring
            wload(t_[:], w_in[l][:, col0:col0 + 128].rearrange("(kc p) c -> p kc c", p=128), b_, nobar=True)

        def proj(wt, bw, ti, pbank, bpb):
            c0, c1 = TILES[ti]
            n = c1 - c0
            for kc in range(8):
                mm(pbank[:, 0:n], wt[:, kc, :], XN[:, kc, c0:c1], kc == 0, kc == 7, [bw, bXN[kc][ti]], [bpb])

        def to_tokmajor_bf(src_bf, bsrc, n, dstT, bdst):
            nb = (n + 127) // 128
            for q in range(nb):
                m = min(128, n - q * 128)
                tr(psb[0:m, q * 128:(q + 1) * 128], src_bf[:, q * 128:q * 128 + m], ident_b[:], [bsrc, bC], [bPSB])
            m = min(128, n)
            cp("act", dstT[0:m, 0:nb, :], psb[0:m, 0:nb * 128].rearrange("p (q c) -> p q c", c=128), [bPSB], [bdst])

        class LA:
            pass

        def la_alloc(pes, need_od=True):
            w = LA()
            w.qd = sbt(pes, [128, 512], BF16)
            w.kt = sbt(pes, [128, 512], BF16)
            w.kd = sbt(pes, [128, 512], BF16)
            w.vb = sbt(pes, [128, 512], BF16)
            w.kdT = sbt(pes, [128, 4, 128], BF16)
            w.vT = sbt(pes, [128, 4, 128], BF16)
            w.PV = sbt(pes, [128, 2048], BF16)
            w.P = w.PV[:, 0:1024].rearrange("p (h t) -> p h t", h=2)
            w.Sf = sbt(pes, [128, 64], F32)
            w.Sb = sbt(pes, [128, 2, 64], BF16)
            w.qd2 = sbt(pes, [128, 2, 512], BF16)
            w.kx = sbt(pes, [128, 512], BF16)
            w.ep = sbt(pes, [128, 4], F32)
            w.sqb2 = sbt(pes, [128, 2, NS], BF16)
            w.sg = sbt(pes, [128, 512], F32)
            w.sq = sbt(pes, [128, 512], F32)
            w.el = sbt(pes, [128, 8], F32)
            w.osq = sbt(pes, [128, 512], BF16)
            w.rst = sbt(pes, [128, 512], F32)
            w.od = sbt(pes, [128, 512], F32) if need_od else None
            w.S0 = sbt(pes, [128, NS, 64], F32)
            w.Snb = sbt(pes, [128, NS, 64], BF16)
            w.vbd = w.PV[0:NS, :].rearrange("p (h b v) -> p h b v", h=2, b=NS)
            w.sqb = sbt(pes, [128, NS], BF16)
            for nm in ["qd", "kt", "kd", "vb", "kdT", "vT", "P", "Sf", "Sb", "sg", "sq", "el", "osq", "rst", "od",
                       "S0", "Snb", "vbd", "sqb", "qd2", "kx", "ep", "sqb2"]:
                setattr(w, "b_" + nm, Buf(nm))
            w.Sn = w.S0
            w.b_Sn = w.b_S0
            w.b_vbd = w.b_P
            return w

        PS_S, PS_U, PS_O, PS_N = 4, 5, 6, 4

        def la_alloc2(pes, need_od=True):
            w0 = la_alloc(pes, need_od)
            w1 = LA()
            w1.__dict__.update(w0.__dict__)
            for nm, shape, dt in [("qd", [128, 512], BF16), ("qd2", [128, 2, 512], BF16), ("kt", [128, 512], BF16),
                                  ("kx", [128, 512], BF16), ("kdT", [128, 4, 128], BF16), ("vT", [128, 4, 128], BF16),
                                  ("ep", [128, 4], F32), ("sg", [128, 512], F32)]:
                setattr(w1, nm, sbt(pes, shape, dt))
                setattr(w1, "b_" + nm, Buf(nm))
            return (w0, w1)

        def la_prep(w, qf, bqf, ktf, bktf, kdf, bkdf, el, bel):
            v4 = lambda ap: ap.rearrange("p (q r j) -> p q r j", q=4, r=2)
            el_e = el[:, 0:8].rearrange("p (q r) -> p q r", r=2)[:, :, 0:1].to_broadcast([128, 4, 64])
            el_o = el[:, 0:8].rearrange("p (q r) -> p q r", r=2)[:, :, 1:2].to_broadcast([128, 4, 64])
            tt("dve", w.qd2[:], qf.unsqueeze(1).to_broadcast([128, 2, 512]),
               hmask[:].unsqueeze(2).to_broadcast([128, 2, 512]), ALU.mult, [bqf, bC], [w.b_qd2])
            cp("act", v4(w.qd[:])[:, :, 0, :], v4(qf)[:, :, 0, :], [bqf], [w.b_qd])
            tt("dve", v4(w.qd[:])[:, :, 1, :], v4(qf)[:, :, 1, :], el_e, ALU.mult, [bqf, bel], [w.b_qd])
            cp("act", w.kt[:], ktf, [bktf], [w.b_kt])
            cp("act", v4(w.kx[:])[:, :, 0, :], v4(kdf)[:, :, 0, :], [bkdf], [w.b_kx])
            cp("act", v4(w.kx[:])[:, :, 1, :], v4(ktf)[:, :, 1, :], [bktf], [w.b_kx])
            tt("dve", v4(w.kd[:])[:, :, 0, :], v4(kdf)[:, :, 0, :], el_o, ALU.mult, [bkdf, bel], [w.b_kd])
            cp("act", v4(w.kd[:])[:, :, 1, :], v4(kdf)[:, :, 1, :], [bkdf], [w.b_kd])
            tt("dve", w.ep[:], el[:, 0:8].rearrange("p (q r) -> p q r", r=2)[:, :, 0],
               el[:, 0:8].rearrange("p (q r) -> p q r", r=2)[:, :, 1], ALU.mult, [bel], [w.b_ep])

        def la_prompt_tile(w, ti, first):
            pS, bpS = ps[PS_S], bPS[PS_S]
            pU, pO = ps[PS_U], ps[PS_O]
            pU3 = pU[:, 0:256].rearrange("p (c v) -> p c v", v=64)

            def scores(hh):
                for pr in range(4):
                    b0 = pr * 128
                    mm(pS[:, b0:b0 + 64], w.kt[:, b0:b0 + 128], w.qd2[:, hh, b0:b0 + 64], True, True,
                       [w.b_kt, w.b_qd2], [bpS])
                    mm(pS[:, b0 + 64:b0 + 128], w.kx[:, b0:b0 + 128], w.qd2[:, hh, b0 + 64:b0 + 128], True, True,
                       [w.b_kx, w.b_qd2], [bpS])
                tt("dve", w.P[:, hh, :].rearrange("p (q t) -> p q t", t=128),
                   pS[:].rearrange("p (q t) -> p q t", t=128),
                   cmask[:].unsqueeze(1).to_broadcast([128, 4, 128]), ALU.mult, [bpS, bC], [w.b_P])

            scores(0)
            for pr in range(4):
                for hh in range(2):
                    mm(pU3[64 * hh:64 * hh + 64, pr, :], w.kdT[:, pr, 64 * hh:64 * hh + 64],
                       w.vT[:, pr, 64 * hh:64 * hh + 64], True, True, [w.b_kdT, w.b_vT], [bPS[PS_U]],
                       tp=(0, 64 * hh))
            scores(1)
            for pr in range(4):
                b0 = pr * 128
                f0 = first and pr == 0
                for hh in range(2):
                    o_ap = pO[64 * hh:64 * hh + 64, b0:b0 + 128]
                    mm(o_ap, w.vT[:, pr, 64 * hh:64 * hh + 64], w.P[:, hh, b0:b0 + 128], True, f0,
                       [w.b_vT, w.b_P], [bPS[PS_O]], tp=(0, 64 * hh))
                    if not f0:
                        mm(o_ap, w.Sb[:, hh, :], w.qd[:, b0:b0 + 128], False, True, [w.b_Sb, w.b_qd], [bPS[PS_O]],
                           tp=(0, 64 * hh))
                if f0:
                    cp("dve", w.Sf[:], pU3[:, pr, :], [bPS[PS_U]], [w.b_Sf])
                else:
                    stt("dve", w.Sf[:], w.Sf[:], w.ep[:, pr:pr + 1], pU3[:, pr, :], ALU.mult, ALU.add,
                        [w.b_Sf, bPS[PS_U], w.b_ep], [w.b_Sf])
                tt("dve", w.Sb[:], w.Sf[:].unsqueeze(1).to_broadcast([128, 2, 64]),
                   hmask[:].unsqueeze(2).to_broadcast([128, 2, 64]), ALU.mult, [w.b_Sf, bC], [w.b_Sb])

        def la_pipeline(regs):
            def region():
                for (X, Y, post, pre) in regs:
                    pre()
                    for t in range(4):
                        X(t)
                        Y(t)
                    X(4)
                    post()
            S.schedule(S.capture(region))

        def la_sample(w, st_in, st_out, l, hp, sq_ap, e1_fn):
            pA, pB, pO = ps[PS_S], ps[PS_U], ps[PS_O]
            dma("sp", w.S0[:], st_in[l, :, 2 * hp:2 * hp + 2, :, :].rearrange("b h k v -> (h k) b v"),
                writes=[w.b_S0])
            e1_fn(w)
            vsrc = w.vT[0:NS, 0, :].rearrange("p (h v) -> p h v", h=2).unsqueeze(2).to_broadcast([NS, 2, NS, 64])
            isrc = ident_f[0:NS, 0:NS].unsqueeze(1).unsqueeze(3).to_broadcast([NS, 2, NS, 64])
            tt("dve", w.vbd[:], vsrc, isrc, ALU.mult, [w.b_vT, bC], [w.b_vbd])
            for hh in range(2):
                for jb, pbank in enumerate((pA, pB)):
                    mm(pbank[64 * hh:64 * hh + 64, :], w.kdT[0:NS, 0, 64 * hh:64 * hh + 64],
                       w.vbd[:, hh, 8 * jb:8 * jb + 8, :].rearrange("p b v -> p (b v)"), True, True,
                       [w.b_kdT, w.b_vbd], [bPS[PS_S] if jb == 0 else bPS[PS_U]], tp=(0, 64 * hh))
            for jb, (pbank, bpb) in enumerate(((pA, bPS[PS_S]), (pB, bPS[PS_U]))):
                tt("dve", w.Sn[:, 8 * jb:8 * jb + 8, :], w.Sn[:, 8 * jb:8 * jb + 8, :],
                   pbank[:].rearrange("p (b v) -> p b v", v=64), ALU.add, [w.b_Sn, bpb], [w.b_Sn])
            dma("sp", st_out[l, :, 2 * hp:2 * hp + 2, :, :].rearrange("b h k v -> (h k) b v"), w.Sn[:],
                reads=[w.b_Sn])
            cp("act", w.Snb[:], w.Sn[:], [w.b_Sn], [w.b_Snb])
            tt("dve", w.sqb2[:], sq_ap.unsqueeze(1).to_broadcast([128, 2, NS]),
               hmask[:].unsqueeze(2).to_broadcast([128, 2, NS]), ALU.mult, [w.b_sq, bC], [w.b_sqb2])
            for b in range(NS):
                for hh in range(2):
                    mm(pO[64 * hh:64 * hh + 64, b:b + 1], w.Snb[:, b, :],
                       w.sqb2[:, hh, b:b + 1], True, True, [w.b_Snb, w.b_sqb2], [bPS[PS_O]], tp=(0, 64 * hh))

        def la_finish(w, n, gcol, dst, bdst, group_norm):
            pO, pN = ps[PS_O], ps[PS_N]
            if group_norm:
                cp("act", w.od[:, 0:n], pO[:, 0:n], [bPS[PS_O]], [w.b_od])
                cp("dve", w.osq[:, 0:n], w.od[:, 0:n], [w.b_od], [w.b_osq])
                mm(pN[:, 0:n], bones_b[:], w.osq[:, 0:n], True, True, [w.b_osq, bC], [bPS[PS_N]])
                stt("dve", w.od[:, 0:n], pN[:, 0:n], -1.0 / 64, w.od[:, 0:n], ALU.mult, ALU.add,
                    [bPS[PS_N], w.b_od], [w.b_od])
                src, bsrc = w.od[:, 0:n], w.b_od
            else:
                src, bsrc = pO[:, 0:n], bPS[PS_O]
            act(w.osq[:, 0:n], src, AF.Square, [bsrc], [w.b_osq])
            mm(pN[:, 0:n], bones_b[:], w.osq[:, 0:n], True, True, [w.b_osq, bC], [bPS[PS_N]])
            act(w.rst[:, 0:n], pN[:, 0:n], AF.Ln, [bPS[PS_N]], [w.b_rst], scale=1.0 / 64, bias=EPS)
            act(w.rst[:, 0:n], w.rst[:, 0:n], AF.Exp, [w.b_rst], [w.b_rst], scale=-0.5)
            stt("dve", w.rst[:, 0:n], src, gcol, w.rst[:, 0:n], ALU.mult, ALU.mult, [bsrc, w.b_rst, bC], [w.b_rst])
            tt("dve", dst, w.rst[:, 0:n], w.sg[:, 0:n], ALU.mult, [w.b_rst, w.b_sg], [bdst])

        def mixer(l):
            with ExitStack() as pes:
                OT = sbt(pes, [128, 8, NT], BF16)
                bOT = [[Buf() for _ in TILES] for _ in range(8)]
                ring = [(sbt(pes, [128, 8, 128], BF16), Buf()) for _ in range(8)]
                rcnt = [0]

                def nextw(col0):
                    r_ = ring[rcnt[0] % 8]
                    rcnt[0] += 1
                    load_wchunk(r_, l, col0)
                    return r_

                preA = [nextw(s_ * 256) for s_ in range(4)]
                rmsnorm_es = ExitStack()
                with rmsnorm_es:
                    rmsnorm(rmsnorm_es, 3 * l + 1)
                    barrier()

                with ExitStack() as bes:
                    ws = la_alloc2(bes, need_od=True)
                    Tp = [[sbt(bes, [128, 512], F32) for _ in range(5)] for _ in range(2)]
                    pass
                    bTp = [[Buf() for _ in range(5)] for _ in range(2)]
                    regsA = []
                    wh = {}
                    for hp in range(2):
                        if hp == 0:
                            wh[("A", 0)] = preA

                        def preA_fn(hp=hp):
                            if hp == 1:
                                wh[("A", 1)] = [nextw(s_ * 256 + 128) for s_ in range(4)]
                        lb = der[:, l, 0, hp:hp + 1]
                        oml = der[:, l, 1, hp:hp + 1]
                        noml = der[:, l, 2, hp:hp + 1]

                        def XA(ti, hp=hp, lb=lb, oml=oml, noml=noml):
                            wq, wf, wi, wg = wh[("A", hp)]
                            w = ws[(ti + hp) % 2]
                            T, bT = Tp[(ti + hp) % 2], bTp[(ti + hp) % 2]
                            c0, c1 = TILES[ti]
                            n = c1 - c0
                            Lc = 64 if ti < 4 else 1
                            proj(wf[0], wf[1], ti, ps[0], bPS[0])
                            proj(wq[0], wq[1], ti, ps[1], bPS[1])
                            proj(wi[0], wi[1], ti, ps[2], bPS[2])
                            proj(wg[0], wg[1], ti, ps[3], bPS[3])
                            sigm(T[0][:, 0:n], ps[0][:, 0:n], [bPS[0]], [bT[0]])
                            sigm(w.sq[:, 0:n], ps[1][:, 0:n], [bPS[1]], [w.b_sq])
                            sigm(w.sg[:, 0:n], ps[3][:, 0:n], [bPS[3]], [w.b_sg])
                            cp("act", w.vb[:, 0:n], ps[2][:, 0:n], [bPS[2]], [w.b_vb])
                            tt("dve", w.sq[:, 0:n], w.sq[:, 0:n], ps[1][:, 0:n], ALU.mult, [w.b_sq, bPS[1]], [w.b_sq])
                            tt("dve", w.sg[:, 0:n], w.sg[:, 0:n], ps[3][:, 0:n], ALU.mult, [w.b_sg, bPS[3]], [w.b_sg])
                            ts("dve", T[1][:, 0:n], T[0][:, 0:n], oml, lb, ALU.mult, ALU.add, [bT[0], bC], [bT[1]])
                            act(T[1][:, 0:n], T[1][:, 0:n], AF.Ln, [bT[1]], [bT[1]], scale=1.0, bias=1e-30)
                            msk = rmask[:, 0:n] if ti < 4 else zmask[:, 0:n]
                            scan(T[2][:, 0:n], msk, T[1][:, 0:n], 0.0, [bT[1], bC], [bT[2]])
                            ts("dve", T[0][:, 0:n], T[0][:, 0:n], noml, oml, ALU.mult, ALU.add, [bT[0], bC], [bT[0]])
                            ts("dve", T[3][:, 0:n], T[2][:, 0:n], -1.0, 80.0, ALU.mult, ALU.min, [bT[2]], [bT[3]])
                            c3 = T[2][:, 0:n].rearrange("p (c j) -> p c j", j=Lc)
                            tt("dve", T[4][:, 0:n].rearrange("p (c j) -> p c j", j=Lc), c3,
                               c3[:, :, Lc - 1:Lc].to_broadcast([128, n // Lc, Lc]), ALU.subtract, [bT[2]], [bT[4]])
                            act(T[2][:, 0:n], T[2][:, 0:n], AF.Exp, [bT[2]], [bT[2]])
                            act(T[3][:, 0:n], T[3][:, 0:n], AF.Exp, [bT[3]], [bT[3]])
                            act(T[4][:, 0:n], T[4][:, 0:n], AF.Exp, [bT[4]], [bT[4]], scale=-1.0)
                            if ti < 4:
                                cp("act", w.el[:, 0:8], T[2][:, 0:n].rearrange("p (c j) -> p c j", j=64)[:, :, 63],
                                   [bT[2]], [w.b_el])
                                tt("dve", T[1][:, 0:n], w.sq[:, 0:n], T[2][:, 0:n], ALU.mult, [w.b_sq, bT[2]], [bT[1]])
                                tt("dve", T[3][:, 0:n], T[0][:, 0:n], T[3][:, 0:n], ALU.mult, [bT[0], bT[3]], [bT[3]])
                                tt("dve", T[4][:, 0:n], T[0][:, 0:n], T[4][:, 0:n], ALU.mult, [bT[0], bT[4]], [bT[4]])
                                la_prep(w, T[1][:, 0:n], bT[1], T[3][:, 0:n], bT[3], T[4][:, 0:n], bT[4], w.el, w.b_el)
                            else:
                                tt("dve", w.kd[:, 0:n], T[0][:, 0:n], T[4][:, 0:n], ALU.mult, [bT[0], bT[4]], [w.b_kd])
                            to_tokmajor_bf(w.kd, w.b_kd, n, w.kdT, w.b_kdT)
                            to_tokmajor_bf(w.vb, w.b_vb, n, w.vT, w.b_vT)

                        def YA(ti, hp=hp):
                            w = ws[(ti + hp) % 2]
                            c0, c1 = TILES[ti]
                            la_prompt_tile(w, ti, ti == 0)
                            if ti == 3:
                                dma("sp", hgrn_p[l, 2 * hp:2 * hp + 2, :, :].rearrange("h k v -> (h k) v"),
                                    w.Sf[:], reads=[w.b_Sf])
                            la_finish(w, c1 - c0, prm[:, l, P_HG, hp:hp + 1], OT[:, hp, c0:c1], bOT[hp][ti], False)

                        def postA(hp=hp):
                            w = ws[hp % 2]
                            T, bT = Tp[hp % 2], bTp[hp % 2]
                            c0, c1 = TILES[4]
                            n = c1 - c0

                            def e1(w_, n=n):
                                tt("dve", w_.Sn[:], w_.S0[:], T[2][:, 0:n].unsqueeze(2).to_broadcast([128, NS, 64]),
                                   ALU.mult, [w_.b_S0, bT[2]], [w_.b_Sn])
                            la_sample(w, st_hgrn, hgrn_s, l, hp, w.sq[:, 0:n], e1)
                            la_finish(w, n, prm[:, l, P_HG, hp:hp + 1], OT[:, hp, c0:c1], bOT[hp][4], False)

                        regsA.append((XA, YA, postA, preA_fn))
                    QKp = [[Tp[0][0], Tp[0][1]], [Tp[1][0], Tp[1][1]]]
                    bQKp = [[bTp[0][0], bTp[0][1]], [bTp[1][0], bTp[1][1]]]
                    QS, KS = Tp[0][2], Tp[1][2]
                    bQS, bKS = bTp[0][2], bTp[1][2]
                    tb = [Tp[0][3], Tp[0][4], Tp[1][3], Tp[1][4]]
                    btb = [bTp[0][3], bTp[0][4], bTp[1][3], bTp[1][4]]
                    regsC = []
                    pass
                    for hp in range(2):
                        def preC_fn(hp=hp):
                            wh[("C", hp)] = [nextw(1536 + s_ * 256 + 128 * hp) for s_ in range(4)]

                        def XC(ti, hp=hp):
                            wq, wk, wv, wg = wh[("C", hp)]
                            w = ws[(ti + hp) % 2]
                            Q, K = QKp[(ti + hp) % 2]
                            bQ, bK = bQKp[(ti + hp) % 2]
                            c0, c1 = TILES[ti]
                            n = c1 - c0
                            for q in range(4):
                                dma_in("sp", tb[q][:, 0:n], c_rtab[hp, q, :, c0:c1], [btb[q]])
                            proj(wq[0], wq[1], ti, ps[0], bPS[0])
                            proj(wk[0], wk[1], ti, ps[1], bPS[1])
                            proj(wv[0], wv[1], ti, ps[2], bPS[2])
                            proj(wg[0], wg[1], ti, ps[3], bPS[3])
                            sigm(w.sg[:, 0:n], ps[3][:, 0:n], [bPS[3]], [w.b_sg])
                            cp("act", w.vb[:, 0:n], ps[2][:, 0:n], [bPS[2]], [w.b_vb])
                            cp("act", Q[:, 0:n], ps[0][:, 0:n], [bPS[0]], [bQ])
                            cp("act", K[:, 0:n], ps[1][:, 0:n], [bPS[1]], [bK])
                            tt("dve", w.sg[:, 0:n], w.sg[:, 0:n], ps[3][:, 0:n], ALU.mult, [w.b_sg, bPS[3]], [w.b_sg])
                            mm(ps[0][:, 0:n], permf[:], Q[:, 0:n], True, True, [bQ, bC], [bPS[0]])
                            mm(ps[1][:, 0:n], permf[:], K[:, 0:n], True, True, [bK, bC], [bPS[1]])
                            tt("dve", Q[:, 0:n], Q[:, 0:n], tb[0][:, 0:n], ALU.mult, [bQ, btb[0]], [bQ])
                            tt("dve", QS[:, 0:n], ps[0][:, 0:n], tb[1][:, 0:n], ALU.mult, [bPS[0], btb[1]], [bQS])
                            tt("dve", Q[:, 0:n], Q[:, 0:n], QS[:, 0:n], ALU.add, [bQ, bQS], [bQ])
                            tt("dve", K[:, 0:n], K[:, 0:n], tb[2][:, 0:n], ALU.mult, [bK, btb[2]], [bK])
                            tt("dve", KS[:, 0:n], ps[1][:, 0:n], tb[3][:, 0:n], ALU.mult, [bPS[1], btb[3]], [bKS])
                            tt("dve", K[:, 0:n], K[:, 0:n], KS[:, 0:n], ALU.add, [bK, bKS], [bK])
                            if ti < 4:
                                ts("dve", KS[:, 0:n], K[:, 0:n], gam[:, hp, 0:1], None, ALU.mult, None, [bK, bC], [bKS])
                                if ti == 0:
                                    memset(w.el[:], 1.0, [w.b_el])
                                    ts("dve", w.el[:], w.el[:], gam[:, hp, 0:1], None, ALU.mult, None, [w.b_el, bC],
                                       [w.b_el])
                                la_prep(w, Q[:, 0:n], bQ, K[:, 0:n], bK, KS[:, 0:n], bKS, w.el, w.b_el)
                            else:
                                ts("dve", w.kd[:, 0:n], K[:, 0:n], gam[:, hp, 1:2], None, ALU.mult, None, [bK, bC],
                                   [w.b_kd])
                                ts("dve", w.sq[:, 0:n], Q[:, 0:n], gam[:, hp, 2:3], None, ALU.mult, None,
                                   [bQ, bC], [w.b_sq])
                            to_tokmajor_bf(w.kd, w.b_kd, n, w.kdT, w.b_kdT)
                            to_tokmajor_bf(w.vb, w.b_vb, n, w.vT, w.b_vT)

                        def YC(ti, hp=hp):
                            w = ws[(ti + hp) % 2]
                            c0, c1 = TILES[ti]
                            la_prompt_tile(w, ti, ti == 0)
                            if ti == 3:
                                dma("sp", ret_p[l, 2 * hp:2 * hp + 2, :, :].rearrange("h k v -> (h k) v"),
                                    w.Sf[:], reads=[w.b_Sf])
                            la_finish(w, c1 - c0, prm[:, l, P_RN, hp:hp + 1], OT[:, 4 + hp, c0:c1], bOT[4 + hp][ti],
                                      True)

                        def postC(hp=hp):
                            w = ws[hp % 2]
                            c0, c1 = TILES[4]
                            n = c1 - c0

                            def e1(w_, hp=hp):
                                ts("dve", w_.Sn[:], w_.S0[:], gam[:, hp, 1:2], None, ALU.mult, None,
                                   [w_.b_S0, bC], [w_.b_Sn])
                            la_sample(w, st_ret, ret_s, l, hp, w.sq[:, 0:n], e1)
                            la_finish(w, n, prm[:, l, P_RN, hp:hp + 1], OT[:, 4 + hp, c0:c1], bOT[4 + hp][4], True)

                        regsC.append((XC, YC, postC, preC_fn))
                    la_pipeline(regsA + regsC)
                    preB = [nextw(1024), nextw(1280)]
                    preD = [nextw(2560)]
                barrier()
                if stop_after == ("mixC", l):
                    return None

                def make_B(bes):
                    UX = sbt(bes, [128, 3 + 512], F32)
                    bUX = Buf()
                    Tp = [[sbt(bes, [128, 512], F32) for _ in range(6)] for _ in range(2)]
                    bTp = [[Buf() for _ in range(6)] for _ in range(2)]
                    xcbp = [sbt(bes, [128, 512], BF16) for _ in range(2)]
                    bxcbp = [Buf() for _ in range(2)]
                    wr = sbt(bes, [128, 128], BF16)
                    wi_ = sbt(bes, [128, 128], BF16)
                    bwr = Buf()
                    hprev = sbt(bes, [128, 1], F32)
                    bhp_ = Buf()
                    hist = sbt(bes, [NS * 3, 128], F32)
                    histT = sbt(bes, [128, NS * 3], F32)
                    h0 = sbt(bes, [NS, 128], F32)
                    h0T = sbt(bes, [128, NS], F32)
                    bhist = Buf()
                    stg16 = sbt(bes, [NS, 2, 128], F32)
                    bstg16 = Buf()

                    def half(hp):
                        wx, wy = preB if hp == 0 else (nextw(1024 + 128 * hp), nextw(1280 + 128 * hp))
                        memset(wr[:], 0.0, [bwr])
                        memset(wi_[:], 0.0, [bwr])
                        for hh in range(2):
                            wload(wr[64 * hh:64 * hh + 64, 64 * hh:64 * hh + 64], w_rgate[l, 2 * hp + hh], bwr)
                            wload(wi_[64 * hh:64 * hh + 64, 64 * hh:64 * hh + 64], w_igate[l, 2 * hp + hh], bwr)
                        memset(UX[:, 0:3], 0.0, [bUX])
                        memset(hprev[:], 0.0, [bhp_])
                        dma_in("sp", hist[:], st_conv[l, :, :, 128 * hp:128 * hp + 128].rearrange("b j c -> (b j) c"),
                               [bhist])
                        dma_in("sp", h0[:], st_lru[l, :, 128 * hp:128 * hp + 128], [bhist])
                        tr(ps[4][:, 0:48], hist[:], ident_f[0:48, 0:48], [bhist, bC], [bPS[4]])
                        tr(ps[4][:, 64:80], h0[:], ident_f[0:16, 0:16], [bhist, bC], [bPS[4]])
                        cp("act", histT[:], ps[4][:, 0:48], [bPS[4]], [bhist])
                        cp("act", h0T[:], ps[4][:, 64:80], [bPS[4]], [bhist])
                        for ti, (c0, c1) in enumerate(TILES):
                            n = c1 - c0
                            T, bT = Tp[ti % 2], bTp[ti % 2]
                            xcb, bxcb = xcbp[ti % 2], bxcbp[ti % 2]
                            proj(wx[0], wx[1], ti, ps[0], bPS[0])
                            proj(wy[0], wy[1], ti, ps[1], bPS[1])
                            cp("act", UX[:, 3:3 + n], ps[0][:, 0:n], [bPS[0]], [bUX])
                            xc = T[0]
                            ts("dve", xc[:, 0:n], UX[:, 3:3 + n], cw[:, l, 3, hp:hp + 1], prm[:, l, P_CB, hp:hp + 1],
                               ALU.mult, ALU.add, [bUX, bC], [bT[0]])
                            for jj in range(3):
                                if ti < 4:
                                    src = UX[:, jj:jj + n]
                                    rd = [bUX]
                                else:
                                    src = histT[:].rearrange("p (b j) -> p b j", j=3)[:, :, jj]
                                    rd = [bhist]
                                stt("dve", xc[:, 0:n], src, cw[:, l, jj, hp:hp + 1], xc[:, 0:n], ALU.mult, ALU.add,
                                    rd + [bT[0], bC], [bT[0]])
                            cp("act", xcb[:, 0:n], xc[:, 0:n], [bT[0]], [bxcb])
                            mm(ps[2][:, 0:n], wr[:], xcb[:, 0:n], True, True, [bwr, bxcb], [bPS[2]])
                            mm(ps[3][:, 0:n], wi_[:], xcb[:, 0:n], True, True, [bwr, bxcb], [bPS[3]])
                            act(T[1][:, 0:n], ps[2][:, 0:n], AF.Sigmoid, [bPS[2], bC], [bT[1]], scale=1.0,
                                bias=prm[:, l, P_BR, hp:hp + 1])
                            act(T[2][:, 0:n], ps[3][:, 0:n], AF.Sigmoid, [bPS[3], bC], [bT[2]], scale=1.0,
                                bias=prm[:, l, P_BI, hp:hp + 1])
                            act(T[3][:, 0:n], ps[1][:, 0:n], AF.Square, [bPS[1]], [bT[3]])
                            ts("dve", T[3][:, 0:n], T[3][:, 0:n], 0.044715, 1.0, ALU.mult, ALU.add, [bT[3]], [bT[3]])
                            tt("dve", T[3][:, 0:n], T[3][:, 0:n], ps[1][:, 0:n], ALU.mult, [bT[3], bPS[1]], [bT[3]])
                            act(T[3][:, 0:n], T[3][:, 0:n], AF.Sigmoid, [bT[3]], [bT[3]], scale=1.5957691216057308)
                            tt("dve", T[3][:, 0:n], T[3][:, 0:n], ps[1][:, 0:n], ALU.mult, [bT[3], bPS[1]], [bT[3]])
                            act(T[4][:, 0:n], T[1][:, 0:n], AF.Exp, [bT[1], bC], [bT[4]], scale=der[:, l, 4, hp:hp + 1])
                            act(T[1][:, 0:n], T[1][:, 0:n], AF.Exp, [bT[1], bC], [bT[1]], scale=der[:, l, 3, hp:hp + 1])
                            ts("dve", T[4][:, 0:n], T[4][:, 0:n], -1.0, 1.0, ALU.mult, ALU.add, [bT[4]], [bT[4]])
                            act(T[4][:, 0:n], T[4][:, 0:n], AF.Ln, [bT[4]], [bT[4]], scale=1.0, bias=1e-30)
                            act(T[4][:, 0:n], T[4][:, 0:n], AF.Exp, [bT[4]], [bT[4]], scale=0.5)
                            tt("dve", T[2][:, 0:n], T[2][:, 0:n], xc[:, 0:n], ALU.mult, [bT[2], bT[0]], [bT[2]])
                            tt("dve", T[2][:, 0:n], T[2][:, 0:n], T[4][:, 0:n], ALU.mult, [bT[2], bT[4]], [bT[2]])
                            H = T[5]
                            if ti < 4:
                                scan(H[:, 0:n], T[1][:, 0:n], T[2][:, 0:n], hprev[:, 0:1], [bT[1], bT[2], bhp_], [bT[5]])
                                cp("dve", hprev[:], H[:, n - 1:n], [bT[5]], [bhp_])
                                cp("dve", UX[:, 0:3], UX[:, n:n + 3], [bUX], [bUX])
                                if ti == 3:
                                    dma("sp", rglru_p[l, 128 * hp:128 * hp + 128].rearrange("(p o) -> p o", o=1),
                                        hprev[:], reads=[bhp_])
                                    dma("sp", conv_p[l, :, 128 * hp:128 * hp + 128].rearrange("j p -> p j"),
                                        UX[:, 0:3], reads=[bUX], nonc=True)
                            else:
                                tt("dve", H[:, 0:n], T[1][:, 0:n], h0T[:], ALU.mult, [bT[1], bhist], [bT[5]])
                                tt("dve", H[:, 0:n], H[:, 0:n], T[2][:, 0:n], ALU.add, [bT[5], bT[2]], [bT[5]])
                                tr(ps[4][0:NS, 0:128], H[:, 0:NS], ident_f[:], [bT[5], bC], [bPS[4]])
                                tr(ps[4][0:NS, 128:256], UX[:, 3:3 + NS], ident_f[:], [bUX, bC], [bPS[4]])
                                cp("act", stg16[:], ps[4][0:NS, 0:256].rearrange("p (a c) -> p a c", c=128),
                                   [bPS[4]], [bstg16])
                                dma("sp", rglru_s[l, :, 128 * hp:128 * hp + 128], stg16[:, 0, :], reads=[bstg16])
                                dma("sp", conv_s[l, :, 2, 128 * hp:128 * hp + 128], stg16[:, 1, :], reads=[bstg16])
                            tt("dve", OT[:, 2 + hp, c0:c1], H[:, 0:n], T[3][:, 0:n], ALU.mult, [bT[5], bT[3]],
                               [bOT[2 + hp][ti]])

                    def tail():
                        dma("sp", conv_s[l, :, 0:2, :], st_conv[l, :, 1:3, :])
                    return half, tail

                def make_D(bes):
                    UX = sbt(bes, [128, 15 + 512], F32)
                    bUX = Buf()
                    SSp = [[sbt(bes, [128, 15 + 512], F32) for _ in range(4)] for _ in range(2)]
                    bSSp = [Buf() for _ in range(2)]
                    PTp = [sbt(bes, [128, 512], F32) for _ in range(2)]
                    bPTp = [Buf() for _ in range(2)]
                    dfbp = [sbt(bes, [128, 512], BF16) for _ in range(2)]
                    bdfbp = [Buf() for _ in range(2)]
                    wp = sbt(bes, [128, 128], BF16)
                    bwp = Buf()
                    hist = sbt(bes, [120, 2, 128], F32)
                    histT = sbt(bes, [128, NS, 15], F32)
                    bhist = Buf()
                    red = sbt(bes, [128, NS], F32)
                    bred = Buf()
                    stg16 = sbt(bes, [NS, 128], F32)
                    bstg16 = Buf()

                    def half(hp):
                        wx = preD[0] if hp == 0 else nextw(2560 + 128 * hp)
                        memset(wp[:], 0.0, [bwp])
                        for hh in range(2):
                            wload(wp[64 * hh:64 * hh + 64, 64 * hh:64 * hh + 64], w_pool[l, 2 * hp + hh], bwp)
                        memset(UX[:, 0:15], 0.0, [bUX])
                        for a in range(2):
                            dma_in("sp", hist[:, a, :],
                                   st_pool[l, 8 * a:8 * a + 8, :, 128 * hp:128 * hp + 128].rearrange(
                                       "b j c -> (b j) c"), [bhist])
                            tr(ps[6][:, 128 * a:128 * a + 120], hist[:, a, :], ident_f[0:120, 0:120], [bhist, bC],
                               [bPS[6]])
                        cp("act", histT[:].rearrange("p (a b) j -> p a (b j)", a=2),
                           ps[6][:, 0:256].rearrange("p (a r) -> p a r", a=2)[:, :, 0:120], [bPS[6]], [bhist])
                        wins = (2 ** (2 * hp + 1), 2 ** (2 * hp + 2))
                        invw = poolc[:, hp, 0:1]
                        for ti, (c0, c1) in enumerate(TILES):
                            n = c1 - c0
                            S2, S4, S8, S16 = SSp[ti % 2]
                            bSS = bSSp[ti % 2]
                            PT, bPT = PTp[ti % 2], bPTp[ti % 2]
                            dfb, bdfb = dfbp[ti % 2], bdfbp[ti % 2]
                            proj(wx[0], wx[1], ti, ps[5], bPS[5])
                            if ti < 4:
                                cp("act", UX[:, 15:15 + n], ps[5][:, 0:n], [bPS[5]], [bUX])
                                m = 15 + n
                                tt("dve", S2[:, 1:m], UX[:, 1:m], UX[:, 0:m - 1], ALU.add, [bUX], [bSS])
                                tt("dve", S4[:, 3:m], S2[:, 3:m], S2[:, 1:m - 2], ALU.add, [bSS], [bSS])
                                if hp == 1:
                                    tt("dve", S8[:, 7:m], S4[:, 7:m], S4[:, 3:m - 4], ALU.add, [bSS], [bSS])
                                    tt("dve", S16[:, 15:m], S8[:, 15:m], S8[:, 7:m - 8], ALU.add, [bSS], [bSS])
                                for hh in range(2):
                                    srcS = {2: S2, 4: S4, 8: S8, 16: S16}[wins[hh]]
                                    ts("dve", PT[64 * hh:64 * hh + 64, 0:n], srcS[64 * hh:64 * hh + 64, 15:15 + n],
                                       poolc[64 * hh:64 * hh + 64, hp, 0:1], None, ALU.mult, None, [bSS, bC], [bPT])
                                if ti == 0:
                                    tt("dve", PT[:, 0:16], PT[:, 0:16], poolc[:, hp, 1:17], ALU.mult, [bPT, bC], [bPT])
                                tt("dve", dfb[:, 0:n], PT[:, 0:n], UX[:, 15:15 + n], ALU.subtract, [bPT, bUX], [bdfb])
                                if ti == 3:
                                    dma("sp", pool_p[l, :, 128 * hp:128 * hp + 128].rearrange("j p -> p j"),
                                        UX[:, n:n + 15], reads=[bUX], nonc=True)
                                cp("pool", UX[:, 0:15], UX[:, n:n + 15], [bUX, bdfb], [bUX])
                            else:
                                cp("act", UX[:, 15:15 + n], ps[5][:, 0:n], [bPS[5]], [bUX])
                                for hh in range(2):
                                    wv_ = wins[hh]
                                    reduce_add(red[64 * hh:64 * hh + 64, :],
                                               histT[64 * hh:64 * hh + 64, :, 15 - (wv_ - 1):15], [bhist], [bred])
                                tt("dve", PT[:, 0:n], red[:], UX[:, 15:15 + n], ALU.add, [bred, bUX], [bPT])
                                ts("dve", PT[:, 0:n], PT[:, 0:n], invw, None, ALU.mult, None, [bPT, bC], [bPT])
                                tt("dve", dfb[:, 0:n], PT[:, 0:n], UX[:, 15:15 + n], ALU.subtract, [bPT, bUX], [bdfb])
                                tr(ps[6][0:NS, 0:128], UX[:, 15:15 + NS], ident_f[:], [bUX, bC], [bPS[6]])
                                cp("act", stg16[:], ps[6][0:NS, 0:128], [bPS[6]], [bstg16])
                                dma("sp", pool_s[l, :, 14, 128 * hp:128 * hp + 128], stg16[:], reads=[bstg16])
                            mm(ps[6][:, 0:n], wp[:], dfb[:, 0:n], True, True, [bwp, bdfb], [bPS[6]])
                            ts("dve", OT[:, 6 + hp, c0:c1], ps[6][:, 0:n], prm[:, l, P_PS, hp:hp + 1], None, ALU.mult,
                               None, [bPS[6], bC], [bOT[6 + hp][ti]])

                    def tail():
                        dma("sp", pool_s[l, :, 0:14, :], st_pool[l, :, 1:15, :])
                    return half, tail

                with ExitStack() as bes:
                    hB, tB = make_B(bes)
                    hD, tD = make_D(bes)
                    def region():
                        for hp in range(2):
                            hB(hp)
                            hD(hp)
                    S.schedule(S.capture(region))
                    tB()
                    tD()
                    preM = [nextw(2816 + b * D) for b in range(4)]
                barrier()
                if stop_after == ("mixD", l):
                    return None

                with ExitStack() as bes:
                    MG = sbt(bes, [128, 4, NT], BF16)
                    acc = sbt(bes, [128, NT], F32)
                    sgt = [sbt(bes, [128, 512], F32) for _ in range(3)]
                    bsgt = [Buf() for _ in range(3)]
                    wbr = [(sbt(bes, [128, 2, 128], BF16), Buf()) for _ in range(4)]
                    wor = [(sbt(bes, [128, 4, 128], BF16), Buf()) for _ in range(4)]
                    bMG = [[Buf() for _ in TILES] for _ in range(4)]
                    bacc = [Buf() for _ in TILES]
                    cnt = 0
                    wcnt = 0
                    ocnt = 0
                    pocnt = 0
                    for dh in range(2):
                        for dcl in range(4):
                            dc = 4 * dh + dcl
                            for b in range(4):
                                wg_ = preM.pop(0) if preM else nextw(2816 + b * D + dc * 128)
                                wb_ = wbr[wcnt % 4]
                                wcnt += 1
                                wload(wb_[0][:], w_branch[l, b][:, dc * 128:(dc + 1) * 128].rearrange(
                                    "(kc p) c -> p kc c", p=128), wb_[1])
                                for ti, (c0, c1) in enumerate(TILES):
                                    n = c1 - c0
                                    pg_, bpg = ps[(0, 1, 4)[cnt % 3]], bPS[(0, 1, 4)[cnt % 3]]
                                    pp_, bpp = ps[(2, 3, 5)[cnt % 3]], bPS[(2, 3, 5)[cnt % 3]]
                                    sg_, bsg = sgt[cnt % 3], bsgt[cnt % 3]
                                    cnt += 1
                                    proj(wg_[0], wg_[1], ti, pg_, bpg)
                                    for kc in range(2):
                                        mm(pp_[:, 0:n], wb_[0][:, kc, :], OT[:, 2 * b + kc, c0:c1], kc == 0, kc == 1,
                                           [wb_[1], bOT[2 * b + kc][ti]], [bpp])
                                    act(sg_[:, 0:n], pg_[:, 0:n], AF.Sigmoid, [bpg], [bsg])
                                    ba = bacc[ti]
                                    if b == 0:
                                        tt("dve", acc[:, c0:c1], sg_[:, 0:n], pp_[:, 0:n], ALU.mult, [bsg, bpp], [ba])
                                    else:
                                        tt("dve", sg_[:, 0:n], sg_[:, 0:n], pp_[:, 0:n], ALU.mult, [bsg, bpp], [bsg])
                                        if b < 3:
                                            tt("dve", acc[:, c0:c1], acc[:, c0:c1], sg_[:, 0:n], ALU.add,
                                               [ba, bsg], [ba])
                                        else:
                                            tt("dve", MG[:, dcl, c0:c1], acc[:, c0:c1], sg_[:, 0:n], ALU.add,
                                               [ba, bsg], [bMG[dcl][ti]])
                        for d2 in range(8):
                            wo_ = wor[ocnt % 4]
                            ocnt += 1
                            wload(wo_[0][:], w_o[l][512 * dh:512 * dh + 512, d2 * 128:(d2 + 1) * 128].rearrange(
                                "(kc p) c -> p kc c", p=128), wo_[1])
                            for ti, (c0, c1) in enumerate(TILES):
                                n = c1 - c0
                                po, bpo = ps[4 + pocnt % 3], bPS[4 + pocnt % 3]
                                pocnt += 1
                                for kc in range(4):
                                    mm(po[:, 0:n], wo_[0][:, kc, :], MG[:, kc, c0:c1], kc == 0, kc == 3,
                                       [wo_[1], bMG[kc][ti]], [bpo])
                                tt("dve", XT[:, d2, c0:c1], XT[:, d2, c0:c1], po[:, 0:n], ALU.add,
                                   [bXT[d2][ti], bpo], [bXT[d2][ti]])
                barrier()
            return None

        dbg = None
        for l in range(L):
            if stop_after is not None and stop_after == ("start", l):
                break
            ffn(ffn1_up[l], ffn1_down[l], 3 * l + 0)
            if stop_after == ("ffn1", l):
                break
            dbg = mixer(l)
            if stop_after is not None and stop_after[1] == l and stop_after[0].startswith("mix"):
                break
            ffn(ffn2_up[l], ffn2_down[l], 3 * l + 2)
            if stop_after == ("ffn2", l):
                break

        with ExitStack() as pes:
            sq = [sbt(pes, [128, 8, 512], BF16) for _ in range(2)]
            bsq = [Buf() for _ in range(2)]
            rs = [sbt(pes, [128, 512], F32) for _ in range(2)]
            brs = [Buf() for _ in range(2)]
            yt = [sbt(pes, [128, 8, 512], F32) for _ in range(2)]
            byt = [Buf() for _ in range(2)]
            ostg = [sbt(pes, [128, D], F32) for _ in range(4)]
            bostg = [Buf() for _ in range(4)]
            def out_region():
                blk_i = 0
                for ti, (c0, c1) in enumerate(TILES):
                    n = c1 - c0
                    s_, bs_ = sq[ti % 2], bsq[ti % 2]
                    r_, br_ = rs[ti % 2], brs[ti % 2]
                    y_, by_ = yt[ti % 2], byt[ti % 2]
                    pb, bpb = ps[4 + ti % 2], bPS[4 + ti % 2]
                    act(s_[:, :, 0:n], XT[:, :, c0:c1], AF.Square, [bXT[k][ti] for k in range(8)], [bs_])
                    for kc in range(8):
                        mm(pb[:, 0:n], ones_b[:], s_[:, kc, 0:n], kc == 0, kc == 7, [bs_, bC], [bpb])
                    act(r_[:, 0:n], pb[:, 0:n], AF.Ln, [bpb], [br_], scale=1.0 / D, bias=EPS)
                    act(r_[:, 0:n], r_[:, 0:n], AF.Exp, [br_], [br_], scale=-0.5)
                    for kc in range(8):
                        stt("dve", y_[:, kc, 0:n], XT[:, kc, c0:c1], gains[:, 6, kc:kc + 1], r_[:, 0:n], ALU.mult,
                            ALU.mult, [bXT[kc][ti], br_, bC], [by_])
                    for q in range((n + 127) // 128):
                        m = min(128, n - q * 128)
                        os_, bos_ = ostg[blk_i % 4], bostg[blk_i % 4]
                        for g in range(2):
                            pt, bpt = ps[(blk_i * 2 + g) % 4], bPS[(blk_i * 2 + g) % 4]
                            for kk in range(4):
                                kc = g * 4 + kk
                                tr(pt[0:m, kk * 128:(kk + 1) * 128], y_[:, kc, q * 128:q * 128 + m], ident_f[:], [by_, bC],
                                   [bpt])
                            cp("act" if g == 0 else "dve", os_[0:m, g * 512:(g + 1) * 512], pt[0:m, :], [bpt], [bos_])
                        r0 = c0 + q * 128
                        dma("sp", y_out[r0:r0 + m, :], os_[0:m, :], reads=[bos_])
                        blk_i += 1

            S.schedule(S.capture(out_region))
        S.finish()
        S.emit(es)
    return nc


_NC_CACHE = {}


def _get_nc():
    if "nc" not in _NC_CACHE:
        _NC_CACHE["nc"] = build_nc()
    return _NC_CACHE["nc"]


def make_in_maps(inputs):
    consts = _const_tables()
    f32 = lambda a: np.ascontiguousarray(np.asarray(a, dtype=np.float32))
    shared = {}
    for k in ["ffn1_up", "ffn1_down", "w_in", "w_rgate", "w_igate", "w_pool", "w_branch", "w_o", "ffn2_up",
              "ffn2_down"]:
        shared[k] = f32(inputs[k])
    fm8 = lambda v: f32(v).reshape(8, 128).T
    fm2 = lambda v: f32(v).reshape(2, 128).T
    gl = [inputs["ffn1_norm"][0], inputs["mix_norm"][0], inputs["ffn2_norm"][0], inputs["ffn1_norm"][1],
          inputs["mix_norm"][1], inputs["ffn2_norm"][1], inputs["final_norm"]]
    shared["gains_fm"] = np.ascontiguousarray(np.stack([fm8(v) for v in gl], axis=1))
    prm = np.zeros((128, L, 12, 2), np.float32)
    cwf = np.zeros((128, L, 4, 2), np.float32)
    for l in range(L):
        plist = [inputs["lb_logits"][0], inputs["lb_logits"][1], inputs["hgrn_norm"][l], inputs["conv_b"][l],
                 inputs["b_rgate"][l], inputs["b_igate"][l], inputs["lru_lambda"][l], inputs["ret_norm"][l],
                 inputs["pool_scale"][l]]
        for idx, v in enumerate(plist):
            prm[:, l, idx, :] = fm2(v)
        for jj in range(4):
            cwf[:, l, jj, :] = fm2(inputs["conv_w"][l, jj])
    shared["prm_fm"] = prm
    shared["cw_fm"] = cwf
    for k, v in consts.items():
        shared[k] = np.ascontiguousarray(v)
    xp = f32(inputs["x_prompt"])
    xs = f32(inputs["x_sample"])
    in_maps = []
    for i in range(NCORES):
        m = dict(shared)
        sl = slice(NS * i, NS * (i + 1))
        m["xin"] = np.ascontiguousarray(np.concatenate([xp[i], xs[sl, 0, :]], axis=0))
        m["st_hgrn"] = f32(inputs["state_hgrn"][:, sl])
        m["st_lru"] = f32(inputs["state_rglru"][:, sl])
        m["st_conv"] = f32(inputs["state_conv"][:, sl])
        m["st_ret"] = f32(inputs["state_retention"][:, sl])
        m["st_pool"] = f32(inputs["state_pool"][:, sl])
        in_maps.append(m)
    return in_maps


def assemble(results):
    y = [r["y"] for r in results]
    y_prompt = np.stack([a[:NP_] for a in y], axis=0)
    y_sample = np.concatenate([a[NP_:] for a in y], axis=0)[:, None, :]
    outs = [y_prompt, y_sample]
    for k in ["hgrn_p", "rglru_p", "conv_p", "ret_p", "pool_p"]:
        outs.append(np.stack([r[k] for r in results], axis=1))
    for k in ["hgrn_s", "rglru_s", "conv_s", "ret_s", "pool_s"]:
        outs.append(np.concatenate([r[k] for r in results], axis=1))
    return tuple(np.ascontiguousarray(o.astype(np.float32)) for o in outs)


def kernel(**inputs):
    nc = _get_nc()
    in_maps = make_in_maps(inputs)
    res = run_bass_kernel_spmd(nc, in_maps, core_ids=list(range(NCORES)))
    return assemble(res.results)
```
